# Optimizing a Trainium2 kernel written in Bass

```python
import jax, jax.numpy as jnp
from jax import lax
import numpy as np

D_MODEL = 1024
BATCH = 4
SEQ = 4096
DEPTH = 1
DEC_BATCH = 16
DEC_SEQ = 2048
PAST_LEN = 128

HEAD_DIM = 64
DIL_PAIRS = ((128, 1), (512, 4), (2048, 16))
HEADS_PER_DIL = 4
N_ATTN_HEADS = HEADS_PER_DIL * len(DIL_PAIRS)
ATTN_WIDTH = N_ATTN_HEADS * HEAD_DIM
N_GMLP_GROUPS = 4
GMLP_GROUP_DIM = 64
GMLP_WIDTH = N_GMLP_GROUPS * GMLP_GROUP_DIM
GMLP_CHUNK = 128
MIX_WIDTH = ATTN_WIDTH + GMLP_WIDTH
IN_WIDTH = 3 * ATTN_WIDTH + 2 * GMLP_WIDTH
D_FF = ((-(-8 * D_MODEL // 3) + 255) // 256) * 256
N_MOD = 6
RMS_EPS = 1e-6
LN_EPS = 1e-5
NEG_INF = -1e30

kernel_name = "hybrid_dilated_attn_gmlp_encoder"


def rms_norm(x, g):
    x32 = x.astype(jnp.float32)
    y = x32 * lax.rsqrt(jnp.mean(x32 * x32, axis=-1, keepdims=True) + RMS_EPS)
    return (y * g.astype(jnp.float32)).astype(x.dtype)


def alibi_slopes():
    return 2.0 ** (-8.0 * jnp.arange(1, N_ATTN_HEADS + 1, dtype=jnp.float32) / N_ATTN_HEADS)


def dilated_window_attention(q, k, v, dil, n_side, slopes):
    B, S, H, E = q.shape
    L = S // dil
    nb = -(-L // n_side)
    Lp = nb * n_side

    def to_classes(t):
        return t.reshape(B, L, dil, H, E).transpose(0, 2, 1, 3, 4)

    qc, kc, vc = to_classes(q), to_classes(k), to_classes(v)
    qb = jnp.pad(qc, ((0, 0), (0, 0), (0, Lp - L), (0, 0), (0, 0))).reshape(B, dil, nb, n_side, H, E)

    def windows(t):
        tp = jnp.pad(t, ((0, 0), (0, 0), (n_side, Lp - L + n_side), (0, 0), (0, 0)))
        tp = tp.reshape(B, dil, nb + 2, n_side, H, E)
        return jnp.concatenate([tp[:, :, :-2], tp[:, :, 1:-1], tp[:, :, 2:]], axis=3)

    kw, vw = windows(kc), windows(vc)
    a = jnp.arange(n_side)
    c = jnp.arange(3 * n_side)
    blk = jnp.arange(nb)
    rel = c[None, :] - n_side - a[:, None]
    key_idx = blk[:, None] * n_side - n_side + c[None, :]
    valid = (jnp.abs(rel)[None] <= n_side) & ((key_idx >= 0) & (key_idx < L))[:, None, :]
    bias = -slopes[:, None, None] * (dil * jnp.abs(rel)).astype(jnp.float32)[None]

    s = jnp.einsum('bdnqhe,bdnkhe->bdnhqk', qb, kw, preferred_element_type=jnp.float32) * (E ** -0.5)
    s = jnp.where(valid[None, None, :, None], s + bias[None, None, None], NEG_INF)
    lse = jax.nn.logsumexp(s, axis=-1)
    p = jnp.exp(s - lse[..., None])
    o = jnp.einsum('bdnhqk,bdnkhe->bdnqhe', p.astype(v.dtype), vw)
    o = o.reshape(B, dil, Lp, H, E)[:, :, :L].transpose(0, 2, 1, 3, 4).reshape(B, S, H, E)
    lse = lse.transpose(0, 1, 2, 4, 3).reshape(B, dil, Lp, H)[:, :, :L].transpose(0, 2, 1, 3).reshape(B, S, H)
    return o, lse


def chunked_spatial_gating(u, v, w_s, b_s, ln_g):
    B, S, _ = u.shape
    u = jax.nn.gelu(u)
    v = jax.nn.gelu(v).reshape(B, S // GMLP_CHUNK, GMLP_CHUNK, N_GMLP_GROUPS, GMLP_GROUP_DIM)
    v32 = v.astype(jnp.float32)
    mu = jnp.mean(v32, axis=-1, keepdims=True)
    var = jnp.mean(jnp.square(v32 - mu), axis=-1, keepdims=True)
    vn = (v32 - mu) * lax.rsqrt(var + LN_EPS) * ln_g.astype(jnp.float32).reshape(N_GMLP_GROUPS, GMLP_GROUP_DIM)
    sv = jnp.einsum('gts,bnsgc->bntgc', w_s.astype(jnp.float32), vn) \
        + b_s.astype(jnp.float32).T[None, None, :, :, None]
    return u * sv.reshape(B, S, GMLP_WIDTH).astype(u.dtype)


def encoder_layer(x, c, w_ada, b_ada, g_pre_mix, w_in, w_s, b_s, g_gmlp, w_out,
                  g_post_mix, g_pre_ffn, w_gu, w_down, g_post_ffn):
    B, S, D = x.shape
    mod = (jnp.dot(jax.nn.silu(c), w_ada) + b_ada)[:, None, :]
    sh1, sc1, gt1, sh2, sc2, gt2 = jnp.split(mod, N_MOD, axis=-1)

    h = rms_norm(x, g_pre_mix) * (1.0 + sc1) + sh1
    proj = jnp.dot(h, w_in)
    A = ATTN_WIDTH
    q = proj[..., :A].reshape(B, S, N_ATTN_HEADS, HEAD_DIM)
    k = proj[..., A:2 * A].reshape(B, S, N_ATTN_HEADS, HEAD_DIM)
    v = proj[..., 2 * A:3 * A].reshape(B, S, N_ATTN_HEADS, HEAD_DIM)
    gu = proj[..., 3 * A:3 * A + GMLP_WIDTH]
    gv = proj[..., 3 * A + GMLP_WIDTH:]

    slopes = alibi_slopes()
    outs, lses = [], []
    for gi, (window, dil) in enumerate(DIL_PAIRS):
        hs = slice(gi * HEADS_PER_DIL, (gi + 1) * HEADS_PER_DIL)
        o, l = dilated_window_attention(q[:, :, hs], k[:, :, hs], v[:, :, hs], dil,
                                        window // (2 * dil), slopes[hs])
        outs.append(o)
        lses.append(l)
    wts = jax.nn.softmax(jnp.stack(lses, axis=0), axis=0)
    attn = jnp.concatenate([(wts[gi][..., None] * outs[gi]).astype(x.dtype) for gi in range(len(DIL_PAIRS))],
                           axis=2).reshape(B, S, ATTN_WIDTH)
    gm = chunked_spatial_gating(gu, gv, w_s, b_s, g_gmlp)
    mix = jnp.dot(jnp.concatenate([attn, gm], axis=-1), w_out)
    x = x + gt1 * rms_norm(mix, g_post_mix)

    h = rms_norm(x, g_pre_ffn) * (1.0 + sc2) + sh2
    gate, up = jnp.split(jnp.dot(h, w_gu), 2, axis=-1)
    f = jnp.dot(jax.nn.silu(gate) * up, w_down)
    return x + gt2 * rms_norm(f, g_post_ffn)


def setup_inputs(seed: int = 0) -> dict:
    key = jax.random.key(seed)
    ks = jax.random.split(key, 18)
    f32 = jnp.float32
    nrm = lambda k, shape, s: jax.random.normal(k, shape, f32) * s
    gain = lambda k, shape: 1.0 + 0.02 * jax.random.normal(k, shape, f32)
    return {
        "x_prompt": jax.random.normal(ks[0], (BATCH, SEQ, D_MODEL), f32),
        "x_sample": jax.random.normal(ks[1], (DEC_BATCH, DEC_SEQ, D_MODEL), f32),
        "c_prompt": jax.random.normal(ks[2], (BATCH, D_MODEL), f32),
        "c_sample": jax.random.normal(ks[3], (DEC_BATCH, D_MODEL), f32),
        "w_ada": nrm(ks[4], (DEPTH, D_MODEL, N_MOD * D_MODEL), 0.5 * D_MODEL ** -0.5),
        "b_ada": nrm(ks[5], (DEPTH, N_MOD * D_MODEL), 0.02),
        "g_pre_mix": gain(ks[6], (DEPTH, D_MODEL)),
        "w_in": nrm(ks[7], (DEPTH, D_MODEL, IN_WIDTH), D_MODEL ** -0.5),
        "w_s": nrm(ks[8], (DEPTH, N_GMLP_GROUPS, GMLP_CHUNK, GMLP_CHUNK), GMLP_CHUNK ** -0.5),
        "b_s": gain(ks[9], (DEPTH, N_GMLP_GROUPS, GMLP_CHUNK)),
        "g_gmlp": gain(ks[10], (DEPTH, GMLP_WIDTH)),
        "w_out": nrm(ks[11], (DEPTH, MIX_WIDTH, D_MODEL), MIX_WIDTH ** -0.5),
        "g_post_mix": gain(ks[12], (DEPTH, D_MODEL)),
        "g_pre_ffn": gain(ks[13], (DEPTH, D_MODEL)),
        "w_gu": nrm(ks[14], (DEPTH, D_MODEL, 2 * D_FF), D_MODEL ** -0.5),
        "w_down": nrm(ks[15], (DEPTH, D_FF, D_MODEL), D_FF ** -0.5),
        "g_post_ffn": gain(ks[16], (DEPTH, D_MODEL)),
    }


def reference(x_prompt, x_sample, c_prompt, c_sample, w_ada, b_ada, g_pre_mix, w_in, w_s, b_s,
              g_gmlp, w_out, g_post_mix, g_pre_ffn, w_gu, w_down, g_post_ffn):
    y_prompt, y_sample = x_prompt, x_sample
    for l in range(DEPTH):
        params = (w_ada[l], b_ada[l], g_pre_mix[l], w_in[l], w_s[l], b_s[l], g_gmlp[l], w_out[l],
                  g_post_mix[l], g_pre_ffn[l], w_gu[l], w_down[l], g_post_ffn[l])
        y_prompt = encoder_layer(y_prompt, c_prompt, *params)
        y_sample = encoder_layer(y_sample, c_sample, *params)
    return (y_prompt, y_sample)
```

```python
import os
from contextlib import ExitStack

import numpy as np
import concourse.bass as bass
import concourse.mybir as mybir
from concourse.bass_utils import run_bass_kernel_spmd

F32 = mybir.dt.float32
BF16 = mybir.dt.bfloat16
I32 = mybir.dt.int32
AF = mybir.ActivationFunctionType
ALU = mybir.AluOpType
AX = mybir.AxisListType

NCORES = 8
NT = 6144
SEG = 2048
NSEG = 3
D = 1024
KT = 8
CH = 512
NCH = int(os.environ.get('KNCH', NT // CH))
DFF = 2816
MT = DFF // 128
DILS = (1, 4, 16)
NSIDE = 64
RMS_EPS = 1e-6
LN_EPS = 1e-5
MAGIC = 0x5F3759DF


class Res:
    def __init__(self, name, multi=False):
        self.name = name
        self.multi = multi
        self.last_write = None
        self.rd_eng = {}
        self.rd_dma = []
        self.writes_all = []
        self.sem = None
        self.dma_count = 0


class Op:
    __slots__ = ("eng", "fn", "deps", "is_dma", "res", "sem_count", "count", "signal")

    def __init__(self, eng, fn, is_dma=False):
        self.eng = eng
        self.fn = fn
        self.deps = []
        self.is_dma = is_dma
        self.res = None
        self.sem_count = 0
        self.count = 0
        self.signal = False


ENGS = ("sp", "act", "pool", "pe", "dve")
STQ = os.environ.get("KSTQ", "act")


class Sched:
    def __init__(self, nc, es):
        self.nc = nc
        self.es = es
        self.q = {e: [] for e in ENGS}
        self.dma_res = []
        self.pending = {e: [] for e in ENGS}
        self.last_op = {e: None for e in ENGS}
        self.dma_since_barrier = []
        self.deferred = []

    def _track(self, o, reads, writes):
        deps = []
        for r in reads:
            if r.multi:
                deps.extend(r.writes_all)
            elif r.last_write is not None:
                deps.append(r.last_write)
        for w in writes:
            if w.multi:
                continue
            if w.last_write is not None:
                deps.append(w.last_write)
            deps.extend(w.rd_eng.values())
            deps.extend(w.rd_dma)
        for r in reads:
            if r.multi:
                continue
            if o.is_dma:
                r.rd_dma.append(o)
            else:
                r.rd_eng[o.eng] = o
        for w in writes:
            if w.multi:
                w.writes_all.append(o)
            else:
                w.last_write = o
                w.rd_eng = {}
                w.rd_dma = []
        if self.pending[o.eng]:
            deps.extend(self.pending[o.eng])
            self.pending[o.eng] = []
        seen = set()
        out = []
        for d in deps:
            if d is o or id(d) in seen:
                continue
            seen.add(id(d))
            out.append(d)
        o.deps = out
        self.q[o.eng].append(o)
        if not o.is_dma:
            self.last_op[o.eng] = o

    def op(self, eng, fn, r=(), w=()):
        o = Op(eng, fn)
        self._track(o, r, w)
        return o

    def dma(self, eng, out, in_, r=(), w=(), sem=None):
        o = Op(eng, lambda e: e.dma_start(out=out, in_=in_), is_dma=True)
        if sem.sem is None:
            sem.sem = self.es.enter_context(self.nc.semaphore("d_" + sem.name))
            self.dma_res.append(sem)
        sem.dma_count += 16
        o.res = sem
        o.sem_count = sem.dma_count
        self._track(o, r, w)
        self.dma_since_barrier.append(o)
        return o

    def defer_dma(self, *a, **k):
        self.deferred.append((a, k))

    def flush_dma(self):
        for a, k in self.deferred:
            self.dma(*a, **k)
        self.deferred = []

    def barrier(self):
        self.flush_dma()
        deps = [o for o in self.last_op.values() if o is not None] + list(self.dma_since_barrier)
        self.dma_since_barrier = []
        for e in ENGS:
            self.pending[e] = list(self.pending[e]) + deps

    def mm(self, out, lhsT, rhs, start, stop, r, w):
        return self.op("pe", lambda e: e.matmul(out, lhsT=lhsT, rhs=rhs, start=start, stop=stop), r, w)

    def tr(self, out, in_, ident, r, w):
        return self.op("pe", lambda e: e.transpose(out=out, in_=in_, identity=ident), r, w)

    def act(self, out, in_, func, r, w, bias=None, scale=None, accum=None):
        kw = {}
        if bias is not None:
            kw["bias"] = bias
        if scale is not None:
            kw["scale"] = scale
        if accum is not None:
            kw["accum_out"] = accum
        return self.op("act", lambda e: e.activation(out=out, in_=in_, func=func, **kw), r, w)

    def tt(self, eng, out, in0, in1, op, r, w):
        return self.op(eng, lambda e: e.tensor_tensor(out=out, in0=in0, in1=in1, op=op), r, w)

    def ts(self, eng, out, in0, s1, s2, op0, op1, r, w):
        if op1 is None:
            return self.op(eng, lambda e: e.tensor_scalar(out=out, in0=in0, scalar1=s1, scalar2=None, op0=op0), r, w)
        return self.op(eng, lambda e: e.tensor_scalar(out=out, in0=in0, scalar1=s1, scalar2=s2, op0=op0, op1=op1), r, w)

    def stt(self, eng, out, in0, scalar, in1, op0, op1, r, w):
        return self.op(eng, lambda e: e.scalar_tensor_tensor(out=out, in0=in0, scalar=scalar, in1=in1, op0=op0, op1=op1), r, w)

    def red(self, out, in_, r, w):
        return self.op("dve", lambda e: e.tensor_reduce(out=out, in_=in_, axis=AX.X, op=ALU.add), r, w)

    def cp(self, eng, out, in_, r, w):
        if eng == "act":
            return self.op("act", lambda e: e.activation(out=out, in_=in_, func=AF.Copy), r, w)
        return self.op(eng, lambda e: e.tensor_copy(out=out, in_=in_), r, w)

    def recip(self, out, in_, r, w):
        return self.op("dve", lambda e: e.reciprocal(out=out, in_=in_), r, w)

    def memset(self, eng, ap, val, r, w):
        return self.op(eng, lambda e: e.memset(ap, val), r, w)

    def emit(self):
        nc = self.nc
        for e in ENGS:
            for o in self.q[e]:
                for d in o.deps:
                    if not d.is_dma and not (d.eng == e and e == "pe"):
                        d.signal = True
        for e in ENGS:
            c = 0
            for o in self.q[e]:
                if not o.is_dma and o.signal:
                    c += 1
                    o.count = c
        sems = {e: self.es.enter_context(nc.semaphore("s_" + e)) for e in ENGS}
        dma_res = self.dma_res
        q = self.q

        def run(e, eng, final=False):
            waited = {}
            for o in q[e]:
                need = {}
                for d in o.deps:
                    if d.is_dma:
                        s, v = d.res.sem, d.sem_count
                    else:
                        if d.eng == e and e == "pe":
                            continue
                        s, v = sems[d.eng], d.count
                    k = id(s)
                    if k not in need or need[k][1] < v:
                        need[k] = (s, v)
                for k, (s, v) in need.items():
                    if waited.get(k, 0) < v:
                        eng.wait_ge(s, v)
                        waited[k] = v
                ins = o.fn(eng)
                if o.is_dma:
                    ins.then_inc(o.res.sem, 16)
                elif o.signal:
                    ins.then_inc(sems[e], 1)
            if final:
                for rr in dma_res:
                    if waited.get(id(rr.sem), 0) < rr.dma_count:
                        eng.wait_ge(rr.sem, rr.dma_count)

        with nc.Block() as block:
            @block.sync
            def _(eng):
                run("sp", eng, final=True)

            @block.scalar
            def _(eng):
                run("act", eng)

            @block.gpsimd
            def _(eng):
                run("pool", eng)

            @block.tensor
            def _(eng):
                run("pe", eng)

            @block.vector
            def _(eng):
                run("dve", eng)


class Ring:
    def __init__(self, items):
        self.items = items
        self.i = 0

    def nxt(self):
        it = self.items[self.i % len(self.items)]
        self.i += 1
        return it


def bc_last(ap, n):
    return bass.AP(ap.tensor, ap.offset, [list(x) for x in ap.ap] + [[0, n]])


def build_program(debug=False, stop_after=3):
    nc = bass.Bass("TRN2", target_bir_lowering=False)
    es = ExitStack()
    S = Sched(nc, es)

    def din(name, shape, dt=F32):
        return nc.dram_tensor(name, list(shape), dt, kind="ExternalInput").ap()

    scratch_kind = "ExternalOutput" if debug else "Internal"

    xin = din("xc", [NT, D])
    ctin = din("ct", [128, KT, NSEG])
    w_ada = din("w_ada", [D, 6 * D])
    b_ada = din("b_ada", [6 * D])
    gpm_pk = din("g_pre_mix_pk", [128, KT])
    gpf_pk = din("g_pre_ffn_pk", [128, KT])
    g_post_mix = din("g_post_mix", [D])
    g_post_ffn = din("g_post_ffn", [D])
    w_in = din("w_in", [D, 2816])
    w_s = din("w_s", [4, 128, 128])
    b_s = din("b_s", [512])
    g_gmlp = din("g_gmlp", [256])
    w_out = din("w_out", [D, D])
    w_gu = din("w_gu", [D, 2 * DFF])
    w_down = din("w_down", [DFF, D])
    identin = din("ident", [128, 128])
    selin = din("sel", [NSEG, NSEG, 128])
    emask = din("emask", [6, 128, 3 * 2 * 2 * 128])

    yout = nc.dram_tensor("y", [NT, D], F32, kind="ExternalOutput").ap()
    qT_s = nc.dram_tensor("qT_s", [6, 128, NT], BF16, kind=scratch_kind).ap()
    kT_s = nc.dram_tensor("kT_s", [6, 128, NT], BF16, kind=scratch_kind).ap()
    V_s = nc.dram_tensor("V_s", [NT, 768], BF16, kind=scratch_kind).ap()
    mixT_s = nc.dram_tensor("mixT_s", [8, 128, NT], BF16, kind=scratch_kind).ap()
    wgu_s = nc.dram_tensor("wgu_s", [MT, 128, KT * 2 * 128], BF16, kind="Internal").ap()
    rD_s = nc.dram_tensor("rD_s", [128, 2, NT], F32, kind="Internal").ap()

    R_qs = Res("qs", multi=True)
    R_ks = Res("ks", multi=True)
    R_vs = Res("vs", multi=True)
    R_mixs = Res("mixs", multi=True)
    R_wgus = Res("wgus", multi=True)
    R_y = Res("y", multi=True)
    R_rDs = Res("rDs", multi=True)

    def sbuf(stack, name, shape, dt):
        return stack.enter_context(nc.sbuf_tensor("sb_" + name, list(shape), dt))

    def psum(stack, name, shape, dt):
        return stack.enter_context(nc.psum_tensor("pp_" + name, list(shape), dt))

    ident_f = sbuf(es, "ident_f", [128, 128], F32)
    ident_b = sbuf(es, "ident_b", [128, 128], BF16)
    ones_b = sbuf(es, "ones_b", [128, 128], BF16)
    s1 = sbuf(es, "s1", [128, NSEG, KT], F32)
    sh1 = sbuf(es, "sh1", [128, NSEG, KT], F32)
    s2 = sbuf(es, "s2", [128, NSEG, KT], F32)
    sh2 = sbuf(es, "sh2", [128, NSEG, KT], F32)
    gg1 = sbuf(es, "gg1", [128, NSEG, D], BF16)
    gg2 = sbuf(es, "gg2", [128, NSEG, D], BF16)
    R_identf, R_identb, R_ones = Res("identf"), Res("identb"), Res("ones")
    R_s1, R_sh1, R_s2, R_sh2, R_gg1, R_gg2 = (Res(n) for n in ("s1", "sh1", "s2", "sh2", "gg1", "gg2"))

    S.dma("sp", ident_f[:], identin[:, :], w=[R_identf], sem=R_identf)
    S.dma("pool", ident_b[:], identin[:, :], w=[R_identb], sem=R_identb)
    S.memset("dve", ones_b[:], 1.0, [], [R_ones])

    def rsqrt_a(y, m, t, Ry, Rm, Rt):
        S.ts("dve", y.bitcast(I32), m.bitcast(I32), 1, None, ALU.arith_shift_right, None, [Rm], [Ry])
        S.ts("dve", y.bitcast(I32), y.bitcast(I32), -1, MAGIC, ALU.mult, ALU.add, [Ry], [Ry])
        rsqrt_it(y, m, t, Ry, Rm, Rt)

    def rsqrt_it(y, m, t, Ry, Rm, Rt):
        S.tt("dve", t, y, y, ALU.mult, [Ry], [Rt])
        S.tt("dve", t, t, m, ALU.mult, [Rt, Rm], [Rt])
        S.ts("dve", t, t, -0.5, 1.5, ALU.mult, ALU.add, [Rt], [Rt])
        S.tt("dve", y, y, t, ALU.mult, [Ry, Rt], [Ry])

    def rsqrt(y, m, t, Ry, Rm, Rt):
        S.ts("dve", y.bitcast(I32), m.bitcast(I32), 1, None, ALU.arith_shift_right, None, [Rm], [Ry])
        S.ts("dve", y.bitcast(I32), y.bitcast(I32), -1, MAGIC, ALU.mult, ALU.add, [Ry], [Ry])
        for _ in range(2):
            S.tt("dve", t, y, y, ALU.mult, [Ry], [Rt])
            S.tt("dve", t, t, m, ALU.mult, [Rt, Rm], [Rt])
            S.ts("dve", t, t, -0.5, 1.5, ALU.mult, ALU.add, [Rt], [Rt])
            S.tt("dve", y, y, t, ALU.mult, [Ry, Rt], [Ry])

    p1w = ExitStack()
    w_in_sb = sbuf(p1w, "w_in_sb", [128, KT, 2816], BF16)
    R_win = Res("w_in")
    w_in_v = w_in.rearrange("(k p) n -> p k n", p=128)
    for (d0, s0, n) in ((0, 0, 768), (768, 768, 768), (1536, 2304, 256), (1792, 1536, 768), (2560, 2560, 256)):
        for kh in range(2):
            S.dma("pool", w_in_sb[:, kh * 4:(kh + 1) * 4, d0:d0 + n], w_in_v[:, kh * 4:(kh + 1) * 4, s0:s0 + n],
                  w=[R_win], sem=R_win)

    with ExitStack() as ps_:
        ct = sbuf(ps_, "ct", [128, KT, NSEG], F32)
        mod_sb = sbuf(ps_, "mod_sb", [NSEG, 6 * D], F32)
        bada = sbuf(ps_, "bada", [NSEG, 6 * D], F32)
        sel_sb = sbuf(ps_, "sel_sb", [NSEG, NSEG, 128], F32)
        gpm_sb = sbuf(ps_, "gpm_sb", [128, KT], F32)
        gpf_sb = sbuf(ps_, "gpf_sb", [128, KT], F32)
        gpost = [sbuf(ps_, "gpostm", [128, D], F32), sbuf(ps_, "gpostf", [128, D], F32)]
        tmp3 = sbuf(ps_, "tmp3", [128, KT, NSEG], F32)
        wa = [sbuf(ps_, f"wa{i}", [128, KT, 512], F32) for i in range(2)]
        modps = [psum(ps_, f"modps{i}", [128, 512], F32) for i in range(2)]
        modT = psum(ps_, "modT", [128, 4, KT, NSEG], F32)
        R_ct, R_mod, R_bada, R_sel, R_gpm, R_gpf = (Res(n) for n in ("ct", "mod", "bada", "sel", "gpm", "gpf"))
        R_gpost = [Res("gpostm"), Res("gpostf")]
        R_tmp3 = Res("tmp3")
        wa_ring = Ring([(wa[i], Res(f"wa{i}")) for i in range(2)])
        mp_ring = Ring([(modps[i], Res(f"modps{i}")) for i in range(2)])
        R_modT = Res("modT")

        S.dma("sp", ct[:], ctin[:, :, :], w=[R_ct], sem=R_ct)
        S.dma("sp", bada[:], b_ada.partition_broadcast(NSEG), w=[R_bada], sem=R_bada)
        S.dma("sp", sel_sb[:], selin[:, :, :], w=[R_sel], sem=R_sel)
        S.dma("sp", gpm_sb[:], gpm_pk[:, :], w=[R_gpm], sem=R_gpm)
        S.dma("sp", gpf_sb[:], gpf_pk[:, :], w=[R_gpf], sem=R_gpf)
        S.dma("sp", gpost[0][:], g_post_mix.partition_broadcast(128), w=[R_gpost[0]], sem=R_gpost[0])
        S.dma("sp", gpost[1][:], g_post_ffn.partition_broadcast(128), w=[R_gpost[1]], sem=R_gpost[1])
        S.act(ct[:], ct[:], AF.Silu, [R_ct], [R_ct])
        w_ada_v = w_ada.rearrange("(k p) n -> p k n", p=128)
        for cb in range(12):
            wt, Rw = wa_ring.nxt()
            S.dma("sp", wt[:], w_ada_v[:, :, cb * 512:(cb + 1) * 512], w=[Rw], sem=Rw)
            mp, Rmp = mp_ring.nxt()
            for k in range(KT):
                S.mm(mp[0:NSEG, :], ct[:, k, :], wt[:, k, :], k == 0, k == KT - 1, [R_ct, Rw], [Rmp])
            S.tt("dve", mod_sb[:, cb * 512:(cb + 1) * 512], mp[0:NSEG, :], bada[:, cb * 512:(cb + 1) * 512],
                 ALU.add, [Rmp, R_bada], [R_mod])
        for wi, j in enumerate((0, 1, 3, 4)):
            for k in range(KT):
                S.mm(modT[:, wi, k, :], mod_sb[0:NSEG, j * D + k * 128: j * D + (k + 1) * 128],
                     ident_f[0:NSEG, 0:NSEG], True, True, [R_mod, R_identf], [R_modT])
        for (wsh, wsc, sh_t, s_t, g_sb, Rsh, Rs, Rg) in ((0, 1, sh1, s1, gpm_sb, R_sh1, R_s1, R_gpm),
                                                        (2, 3, sh2, s2, gpf_sb, R_sh2, R_s2, R_gpf)):
            S.cp("dve", sh_t[:].rearrange("p s k -> p k s"), modT[:, wsh, :, :], [R_modT], [Rsh])
            S.ts("dve", tmp3[:], modT[:, wsc, :, :], 1.0, None, ALU.add, None, [R_modT], [R_tmp3])
            S.tt("dve", s_t[:].rearrange("p s k -> p k s"), tmp3[:], bc_last(g_sb[:, :], NSEG), ALU.mult,
                 [R_tmp3, Rg], [Rs])
        for s in range(NSEG):
            for gi, (j, gg_t, Rgg) in enumerate(((2, gg1, R_gg1), (5, gg2, R_gg2))):
                for half in range(2):
                    mp, Rmp = mp_ring.nxt()
                    S.mm(mp[:, :], sel_sb[0:NSEG, s, :], mod_sb[0:NSEG, j * D + half * 512: j * D + (half + 1) * 512],
                         True, True, [R_sel, R_mod], [Rmp])
                    S.tt("dve", gg_t[:, s, half * 512:(half + 1) * 512], mp[:, :],
                         gpost[gi][:, half * 512:(half + 1) * 512], ALU.mult, [Rmp, R_gpost[gi]], [Rgg])
        S.barrier()

    wgu_src = w_gu.rearrange("(k p) (g m c) -> m p k g c", p=128, g=2, c=128)
    R_wgu_cvt = Res("wgucvt")
    for m in range(0 if not os.environ.get('KNOWGU') else MT, MT):
        dst = wgu_s[m].rearrange("p (k g c) -> p k g c", k=KT, g=2)
        for g in range(2):
            S.dma("pool", dst[:, :, g, :], wgu_src[m][:, :, g, :], w=[R_wgus], sem=R_wgu_cvt)

    if stop_after < 1:
        S.flush_dma()
        S.emit()
        return nc
    with ExitStack() as p1:
        ws_f = sbuf(p1, "ws_f", [128, 4, 128], F32)
        w_sT = sbuf(p1, "w_sT", [128, 4, 128], BF16)
        bs_f = sbuf(p1, "bs_f", [1, 512], F32)
        bs_hi = sbuf(p1, "bs_hi", [1, 512], BF16)
        bs_hf = sbuf(p1, "bs_hf", [1, 512], F32)
        bs_lo = sbuf(p1, "bs_lo", [1, 512], BF16)
        ggm = sbuf(p1, "ggm", [128, 256], F32)
        R_wsf, R_wsT, R_bsf, R_bshi, R_bshf, R_bslo, R_ggm = (Res(n) for n in
                                                              ("wsf", "wsT", "bsf", "bshi", "bshf", "bslo", "ggm"))
        S.dma("sp", ws_f[:], w_s.rearrange("g t s -> t g s"), w=[R_wsf], sem=R_wsf)
        S.dma("sp", bs_f[:], b_s.partition_broadcast(1), w=[R_bsf], sem=R_bsf)
        S.dma("sp", ggm[:], g_gmlp.partition_broadcast(128), w=[R_ggm], sem=R_ggm)
        S.cp("dve", bs_hi[:], bs_f[:], [R_bsf], [R_bshi])
        S.cp("dve", bs_hf[:], bs_hi[:], [R_bshi], [R_bshf])
        S.tt("dve", bs_lo[:], bs_f[:], bs_hf[:], ALU.subtract, [R_bsf, R_bshf], [R_bslo])

        xr = Ring([(sbuf(p1, f"x{i}", [128, D], F32), Res(f"x{i}")) for i in range(8)])
        xnr = Ring([(sbuf(p1, f"xn{i}", [128, D], BF16), Res(f"xn{i}")) for i in range(8)])
        junk = sbuf(p1, "junk", [128, D], BF16)
        R_junk = Res("junk")
        ssr = Ring([(sbuf(p1, f"ss{i}", [128, 4], F32), Res(f"ss{i}")) for i in range(2)])
        rsr = Ring([(sbuf(p1, f"rs{i}", [128, 4], F32), Res(f"rs{i}")) for i in range(2)])
        rtr = Ring([(sbuf(p1, f"rt{i}", [128, 4], F32), Res(f"rt{i}")) for i in range(2)])
        hTr = Ring([(sbuf(p1, f"hT{i}", [128, KT, CH], BF16), Res(f"hT{i}")) for i in range(2)])
        qkr = Ring([(sbuf(p1, f"qk{i}", [128, 12, CH], BF16), Res(f"qk{i}")) for i in range(2)])
        vstr = Ring([(sbuf(p1, f"vst{i}", [128, 768], BF16), Res(f"vst{i}")) for i in range(4)])
        gur = Ring([(sbuf(p1, f"gu{i}", [128, 2, CH], F32), Res(f"gu{i}")) for i in range(2)])
        gmr = Ring([(sbuf(p1, f"gm{i}", [128, 2, CH], BF16), Res(f"gm{i}")) for i in range(2)])
        gvr = Ring([(sbuf(p1, f"gv{i}", [128, 256], F32), Res(f"gv{i}")) for i in range(8)])
        sqr = Ring([(sbuf(p1, f"sq{i}", [128, 256], F32), Res(f"sq{i}")) for i in range(8)])
        vnr = Ring([(sbuf(p1, f"vn{i}", [128, 256], BF16), Res(f"vn{i}")) for i in range(8)])
        lnr = Ring([(sbuf(p1, f"ln{i}", [128, 5, 16], F32), Res(f"ln{i}")) for i in range(2)])
        hfr = Ring([(sbuf(p1, f"hf{i}", [128, KT, 128], F32), Res(f"hf{i}")) for i in range(2)])
        tpr = Ring([(psum(p1, f"tp{i}", [128, KT, 128], BF16), Res(f"tp{i}")) for i in range(2)])
        fmr = Ring([(psum(p1, f"fm{i}", [128, CH], F32), Res(f"fm{i}")) for i in range(2)])
        tmr = Ring([(psum(p1, f"tm{i}", [128, D], F32), Res(f"tm{i}")) for i in range(2)])

        for g in range(4):
            fm, Rfm = fmr.nxt()
            S.mm(fm[:, 0:128], ws_f[:, g, :], ident_f[:, :], True, True, [R_wsf, R_identf], [Rfm])
            S.cp("dve", w_sT[:, g, :], fm[:, 0:128], [Rfm], [R_wsT])

        def stageA1(c):
            tok0 = c * CH
            ss, Rss = ssr.nxt()
            rs, Rrs = rsr.nxt()
            rt, Rrt = rtr.nxt()
            S.memset("dve", ss[:], 0.0, [], [Rss])
            xs = []
            for i in range(4):
                xt, Rx = xr.nxt()
                S.dma("sp", xt[:], xin[tok0 + i * 128: tok0 + (i + 1) * 128, :], w=[Rx], sem=Rx)
                S.act(junk[:], xt[:], AF.Square, [Rx], [R_junk, Rss], accum=ss[:, i:i + 1])
                xs.append((xt, Rx))
            S.ts("dve", ss[:], ss[:], 1.0 / D, RMS_EPS, ALU.mult, ALU.add, [Rss], [Rss])
            rsqrt(rs[:], ss[:], rt[:], Rrs, Rss, Rrt)
            return (xs, rs, Rrs)

        def stageA1b(pre):
            xs, rs, Rrs = pre
            xns = []
            for i in range(4):
                xt, Rx = xs[i]
                xn, Rxn = xnr.nxt()
                S.act(xn[:], xt[:], AF.Identity, [Rx, Rrs], [Rxn], scale=rs[:, i:i + 1])
                xns.append((xn, Rxn))
            return xns

        def stageA2(c, xns, tiles=(0, 1, 2, 3), h=None):
            seg = c // 4
            hT, RhT = h if h is not None else hTr.nxt()
            for i in tiles:
                xn, Rxn = xns[i]
                tp, Rtp = tpr.nxt()
                for k in range(KT):
                    S.tr(tp[:, k, :], xn[:, k * 128:(k + 1) * 128], ident_b[:], [Rxn, R_identb], [Rtp])
                ho = hT[:, :, i * 128:(i + 1) * 128]
                hf, Rhf = hfr.nxt()
                S.tt("dve", hf[:], tp[:, :, :], bc_last(s1[:, seg, :], 128), ALU.mult, [Rtp, R_s1], [Rhf])
                S.tt("dve", ho, hf[:], bc_last(sh1[:, seg, :], 128), ALU.add, [Rhf, R_sh1], [RhT])
            return hT, RhT

        def fm_tile(c, m, hT, RhT, st):
            qk, Rqk = st["qk"]
            gu, Rgu = st["gu"]
            fm, Rfm = fmr.nxt()
            for k in range(KT):
                S.mm(fm[:, :], w_in_sb[:, k, m * 128:(m + 1) * 128], hT[:, k, :], k == 0, k == KT - 1,
                     [R_win, RhT], [Rfm])
            if m < 12:
                S.cp("act" if m % 2 == 0 else "dve", qk[:, m, :], fm[:, :], [Rfm], [Rqk])
            else:
                S.act(gu[:, m - 12, :], fm[:, :], AF.Gelu_apprx_tanh, [Rfm], [Rgu])
            if m == 11:
                tok0 = c * CH
                S.defer_dma(STQ, qT_s[:, :, tok0:tok0 + CH].rearrange("t p n -> p t n"), qk[:, 0:6, :],
                            r=[Rqk], w=[R_qs], sem=Rqk)
                S.defer_dma(STQ, kT_s[:, :, tok0:tok0 + CH].rearrange("t p n -> p t n"), qk[:, 6:12, :],
                            r=[Rqk], w=[R_ks], sem=Rqk)

        def tm_tile(c, i, hT, RhT, st):
            tok0 = c * CH
            if i == 0:
                st["gm"] = gmr.nxt()
                st["vn"] = {}
            tm, Rtm = tmr.nxt()
            for half in range(2):
                for k in range(KT):
                    S.mm(tm[:, half * 512:(half + 1) * 512], hT[:, k, i * 128:(i + 1) * 128],
                         w_in_sb[:, k, 1792 + half * 512: 1792 + (half + 1) * 512], k == 0, k == KT - 1,
                         [R_win, RhT], [Rtm])
            vst, Rvst = vstr.nxt()
            S.cp("act", vst[:], tm[:, 0:768], [Rtm], [Rvst])
            S.defer_dma(STQ, V_s[tok0 + i * 128: tok0 + (i + 1) * 128, :], vst[:], r=[Rvst], w=[R_vs], sem=Rvst)
            if i == 0:
                st["ln"] = lnr.nxt()
                st["gv"] = {}
            ln, Rln = st["ln"]
            gv, Rgv = gvr.nxt()
            sq, Rsq = sqr.nxt()
            S.act(gv[:], tm[:, 768:1024], AF.Gelu_apprx_tanh, [Rtm, Rvst], [Rgv])
            gv3 = gv[:].rearrange("p (g c) -> p g c", g=4)
            sq3 = sq[:].rearrange("p (g c) -> p g c", g=4)
            S.act(sq[:], gv[:], AF.Square, [Rgv], [Rsq])
            S.red(ln[:, 0, i * 4:(i + 1) * 4], gv3, [Rgv], [Rln])
            S.red(ln[:, 1, i * 4:(i + 1) * 4], sq3, [Rsq], [Rln])
            st["gv"][i] = (gv, Rgv, sq, Rsq)

        def ln_chain(c, st):
            ln, Rln = st["ln"]
            S.ts("dve", ln[:, 0, :], ln[:, 0, :], 1.0 / 64, None, ALU.mult, None, [Rln], [Rln])
            S.tt("dve", ln[:, 2, :], ln[:, 0, :], ln[:, 0, :], ALU.mult, [Rln], [Rln])
            S.stt("dve", ln[:, 1, :], ln[:, 1, :], 1.0 / 64, ln[:, 2, :], ALU.mult, ALU.subtract, [Rln], [Rln])
            S.ts("dve", ln[:, 1, :], ln[:, 1, :], LN_EPS, None, ALU.add, None, [Rln], [Rln])
            rsqrt(ln[:, 3, :], ln[:, 1, :], ln[:, 4, :], Rln, Rln, Rln)

        def vn_tile(c, i, st):
            ln, Rln = st["ln"]
            gv, Rgv, sq, Rsq = st["gv"][i]
            vn, Rvn = vnr.nxt()
            gv3 = gv[:].rearrange("p (g c) -> p g c", g=4)
            sq3 = sq[:].rearrange("p (g c) -> p g c", g=4)
            S.tt("dve", sq3, gv3, bc_last(ln[:, 0, i * 4:(i + 1) * 4], 64), ALU.subtract, [Rgv, Rln], [Rsq])
            S.tt("dve", sq3, sq3, bc_last(ln[:, 3, i * 4:(i + 1) * 4], 64), ALU.mult, [Rsq, Rln], [Rsq])
            S.tt("dve", vn[:], sq[:], ggm[:], ALU.mult, [Rsq, R_ggm], [Rvn])
            st["vn"][i] = (vn, Rvn)

        def gm_tile(c, i, st):
            tok0 = c * CH
            gm, Rgm = st["gm"]
            gu, Rgu = st["gu"]
            vn, Rvn = st["vn"][i]
            fm, Rfm = fmr.nxt()
            for g in range(4):
                o = fm[(g % 2) * 64:(g % 2) * 64 + 64, (g // 2) * 128:(g // 2) * 128 + 128]
                S.mm(o, vn[:, g * 64:(g + 1) * 64], w_sT[:, g, :], True, False, [Rvn, R_wsT], [Rfm])
                S.mm(o, ones_b[0:1, 0:64], bs_hi[0:1, g * 128:(g + 1) * 128], False, False, [R_ones, R_bshi], [Rfm])
                S.mm(o, ones_b[0:1, 0:64], bs_lo[0:1, g * 128:(g + 1) * 128], False, True, [R_ones, R_bslo], [Rfm])
            S.tt("dve", gm[:, :, i * 128:(i + 1) * 128], gu[:, :, i * 128:(i + 1) * 128],
                 fm[:, 0:256].rearrange("p (a t) -> p a t", a=2), ALU.mult, [Rgu, Rfm], [Rgm])
            if i == 3:
                S.defer_dma(STQ, mixT_s[6:8, :, tok0:tok0 + CH].rearrange("t p n -> p t n"), gm[:, :, :],
                            r=[Rgm], w=[R_mixs], sem=Rgm)

        hcur = stageA2(0, stageA1b(stageA1(0)))
        for c in range(NCH):
            st = {}
            pre_next = stageA1(c + 1) if c + 1 < NCH else None
            xns_next = None
            hT, RhT = hcur
            order = [("tm", 0), ("tm", 1), ("fm", 12), ("tm", 2), ("fm", 13), ("tm", 3), ("A1b", 0), ("fm", 0), ("ln", 0),
                     ("fm", 1), ("fm", 2), ("vn", 0), ("vn", 1), ("fm", 3), ("fm", 4), ("vn", 2), ("vn", 3),
                     ("fm", 5), ("gm", 0), ("A2", 0), ("fm", 6), ("gm", 1), ("fm", 7), ("A2", 1), ("gm", 2),
                     ("fm", 8), ("gm", 3), ("fm", 9), ("A2", 2), ("fm", 10), ("fm", 11), ("A2", 3)]
            hnext = None
            st["qk"] = qkr.nxt()
            st["gu"] = gur.nxt()
            for kind, idx in order:
                if kind == "fm":
                    fm_tile(c, idx, hT, RhT, st)
                elif kind == "tm":
                    tm_tile(c, idx, hT, RhT, st)
                elif kind == "gm":
                    gm_tile(c, idx, st)
                elif kind == "ln":
                    ln_chain(c, st)
                elif kind == "A1b":
                    xns_next = stageA1b(pre_next) if c + 1 < NCH else None
                elif kind == "vn":
                    vn_tile(c, idx, st)
                else:
                    if c + 1 < NCH:
                        hnext = stageA2(c + 1, xns_next, tiles=(idx,), h=hnext)
                    if idx == 3:
                        S.flush_dma()
            hcur = hnext
        S.barrier()

    p1w.close()
    p3w = ExitStack()
    w_out_sb = sbuf(p3w, "w_out_sb", [128, KT, D], BF16)
    R_wout = Res("wout")
    w_out_v = w_out.rearrange("(k p) n -> p k n", p=128)
    for kh in range(2):
        S.dma("pool", w_out_sb[:, kh * 4:(kh + 1) * 4, :], w_out_v[:, kh * 4:(kh + 1) * 4, :], w=[R_wout], sem=R_wout)
    if stop_after >= 2:
      with ExitStack() as p2:
        qAb = [sbuf(p2, f"qA{i}", [128, NT], BF16) for i in range(2)]
        qBb = [sbuf(p2, f"qB{i}", [128, NT], BF16) for i in range(2)]
        kTb = [sbuf(p2, f"kTb{i}", [128, NT], BF16) for i in range(2)]
        Vcb = [sbuf(p2, f"Vcb{i}", [128, 48, 128], BF16) for i in range(2)]
        mkb = [sbuf(p2, f"mkb{i}", [128, 3, 512], F32) for i in range(2)]
        R_qAb = [Res(f"qA{i}") for i in range(2)]
        R_qBb = [Res(f"qB{i}") for i in range(2)]
        R_kTb = [Res(f"kTb{i}") for i in range(2)]
        R_Vcb = [Res(f"Vcb{i}") for i in range(2)]
        R_mkb = [Res(f"mkb{i}") for i in range(2)]
        R_qAz = [Res(f"qAz{i}") for i in range(2)]
        R_qBz = [Res(f"qBz{i}") for i in range(2)]
        for i in range(2):
            S.memset("dve", qAb[i][64:128, :], 0.0, [], [R_qAz[i]])
            S.op("act", lambda e, ap=qBb[i][0:64, :]: e.memzero(ap), [], [R_qBz[i]])
        numTb = [sbuf(p2, f"numT{i}", [128, NT], BF16) for i in range(2)]
        R_numTb = [Res(f"numT{i}") for i in range(2)]
        Dt = sbuf(p2, "Dt", [128, NT], F32)
        R_D = Res("D")
        pxr = Ring([(sbuf(p2, f"px{i}", [128, 512], F32), Res(f"px{i}")) for i in range(3)])
        pmr = Ring([(sbuf(p2, f"pm{i}", [128, 2, 2, 128], BF16), Res(f"pm{i}")) for i in range(6)])
        scr_ = Ring([(psum(p2, f"sc{i}", [128, 2, 2, 128], F32), Res(f"sc{i}")) for i in range(4)])
        pvr = Ring([(psum(p2, f"pv{i}", [128, 2, 128], F32), Res(f"pv{i}")) for i in range(4)])

        pair_no = 0
        for u in range(2):
            for gi in range(3):
                t = 2 * gi + u
                d = DILS[gi]
                n = 48 // d
                npseg = n // 3
                sl = pair_no % 2
                pair_no += 1
                numT = {gi: numTb[sl]}
                R_numT = {gi: R_numTb[sl]}
                kT, Vc, mk = kTb[sl], Vcb[sl], mkb[sl]
                RkT, RVc, Rmk = R_kTb[sl], R_Vcb[sl], R_mkb[sl]
                qH = (qAb[sl], qBb[sl])
                RqH = (R_qAb[sl], R_qBb[sl])
                RqZ = (R_qAz[sl], R_qBz[sl])
                S.dma("sp", qH[0][0:64, :], qT_s[t][0:64, :], r=[R_qs], w=[RqH[0]], sem=RqH[0])
                S.dma("sp", qH[1][64:128, :], qT_s[t][64:128, :], r=[R_qs], w=[RqH[1]], sem=RqH[1])
                S.dma("sp", kT[:], kT_s[t], r=[R_ks], w=[RkT], sem=RkT)
                Vsrc = V_s.rearrange("(j p r) c -> r p j c", p=128, r=d)
                for r in range(d):
                    for j0 in range(0, n, 8):
                        j1 = min(n, j0 + 8)
                        S.dma(STQ, Vc[:, r * n + j0:r * n + j1, :], Vsrc[r][:, j0:j1, t * 128:(t + 1) * 128],
                              r=[R_vs], w=[RVc], sem=RVc)
                S.dma("sp", mk[:].rearrange("p a b -> p (a b)"), emask[t], w=[Rmk], sem=Rmk)
                S.flush_dma()

                def tokslice(pos0, cnt):
                    return slice(pos0 * d + r, pos0 * d + r + (cnt - 1) * d + 1, d) if d > 1 else slice(pos0, pos0 + cnt)

                prev = None
                pend = []
                LAG = 3
                blocks = [(r, j) for r in range(d) for j in range(n)]
                for bi in range(len(blocks) + LAG):
                    if bi < len(blocks):
                        r, j = blocks[bi]
                        tj = (j, (j + 1) % n)
                        if j + 1 < n:
                            runs = [(0, 128, tokslice(128 * j + 64, 128), tokslice(128 * j + 64, 128))]
                        else:
                            runs = [(0, 64, tokslice(128 * j + 64, 64), tokslice(128 * j + 64, 64)),
                                    (64, 64, tokslice(0, 64), tokslice(0, 64))]
                        if (j + 1) % npseg != 0:
                            mt_ = 0
                        elif j + 1 == npseg:
                            mt_ = 1
                        else:
                            mt_ = 2
                        sc, Rsc = scr_.nxt()
                        for h in range(2):
                            for kt in range(2):
                                ks = tokslice(128 * tj[kt], 128)
                                for (c0, ncol, qs, qcs) in runs:
                                    S.mm(sc[:, h, kt, c0:c0 + ncol], kT[:, ks],
                                         qH[h][:, qcs], True, True, [RkT, RqH[h], RqZ[h]], [Rsc])
                        px, Rpx = pxr.nxt()
                        pm, Rpm = pmr.nxt()
                        S.act(px[:], sc[:].rearrange("p a b c -> p (a b c)"), AF.Exp, [Rsc], [Rpx], scale=0.125)
                        S.tt("dve", pm[:].rearrange("p a b c -> p (a b c)"), px[:], mk[:, mt_, :], ALU.mult,
                             [Rpx, Rmk], [Rpm])
                        pend.append((r, j, tj, runs, pm, Rpm))
                    prev = pend.pop(0) if (len(pend) > LAG or (bi >= len(blocks) and pend)) else None
                    if prev is not None:
                        pr, pj, ptj, pruns, ppm, Rppm = prev
                        pv, Rpv = pvr.nxt()
                        for h in range(2):
                            rows = slice(h * 64, (h + 1) * 64)
                            for kt in range(2):
                                S.mm(pv[rows, 0, :], Vc[:, pr * n + ptj[kt], h * 64:(h + 1) * 64], ppm[:, h, kt, :],
                                     kt == 0, kt == 1, [RVc, Rppm], [Rpv])
                            for kt in range(2):
                                S.mm(pv[rows, 1, :], ones_b[:, 0:64], ppm[:, h, kt, :],
                                     kt == 0, kt == 1, [R_ones, Rppm], [Rpv])
                        for (c0, ncol, qs, _qcs) in pruns:
                            S.cp("act", numT[gi][:, qs], pv[:, 0, c0:c0 + ncol], [Rpv], [R_numT[gi]])
                            if gi == 0:
                                S.cp("dve", Dt[:, qs], pv[:, 1, c0:c0 + ncol], [Rpv, R_numT[gi]], [R_D])
                            else:
                                S.tt("dve", Dt[:, qs], Dt[:, qs], pv[:, 1, c0:c0 + ncol], ALU.add, [R_D, Rpv, R_numT[gi]], [R_D])
                assert not pend
                S.defer_dma(STQ, mixT_s[t], numT[gi][:], r=[R_numT[gi]], w=[R_mixs], sem=R_numT[gi])
            S.flush_dma()
            for q4 in range(4):
                cs = slice(q4 * 1536, (q4 + 1) * 1536)
                S.dma("sp" if q4 % 2 == 0 else STQ, rD_s[:, u, cs], Dt[:, cs], r=[R_D], w=[R_rDs], sem=R_D)
        S.barrier()

    if stop_after >= 3:
      with ExitStack() as p3:
        w_down_sb = sbuf(p3, "w_down_sb", [128, MT, D], BF16)
        R_wdown = Res("wdown")
        w_down_v = w_down.rearrange("(k p) n -> p k n", p=128)
        for kh in range(11):
            S.dma("pool", w_down_sb[:, kh * 2:(kh + 1) * 2, :], w_down_v[:, kh * 2:(kh + 1) * 2, :], w=[R_wdown], sem=R_wdown)
        wgr = Ring([(sbuf(p3, f"wg{i}", [128, KT, 2, 128], BF16), Res(f"wg{i}")) for i in range(4)])
        mtr = Ring([(sbuf(p3, f"mt{i}", [128, KT, CH], BF16), Res(f"mt{i}")) for i in range(1)])
        xr = Ring([(sbuf(p3, f"x3_{i}", [128, D], F32), Res(f"x3_{i}")) for i in range(4)])
        rdc = sbuf(p3, "rdc", [128, 2, CH], F32)
        R_rdc = Res("rdc")
        x1r = Ring([(sbuf(p3, f"x1_{i}", [128, 4, D], F32), Res(f"x1_{i}")) for i in range(2)])
        msr = Ring([(sbuf(p3, f"ms{i}", [128, D], F32), Res(f"ms{i}")) for i in range(2)])
        xnr = Ring([(sbuf(p3, f"xn3_{i}", [128, D], BF16), Res(f"xn3_{i}")) for i in range(4)])
        h2r = Ring([(sbuf(p3, f"h2T{i}", [128, KT, CH], BF16), Res(f"h2T{i}")) for i in range(2)])
        aT = sbuf(p3, "aT", [128, MT, CH], BF16)
        R_aT = Res("aT")
        sgr = Ring([(sbuf(p3, f"sg{i}", [128, CH], BF16), Res(f"sg{i}")) for i in range(2)])
        junk = sbuf(p3, "junk3", [128, D], BF16)
        R_junk = Res("junk3")
        nrm = Ring([(sbuf(p3, f"nrm{i}", [128, 12], F32), Res(f"nrm{i}")) for i in range(6)])
        tmor = Ring([(psum(p3, f"tmo{i}", [128, 512], F32), Res(f"tmo{i}")) for i in range(3)])
        tpr = Ring([(psum(p3, f"tp3_{i}", [128, KT, 128], BF16), Res(f"tp3_{i}")) for i in range(1)])
        gpr = Ring([(psum(p3, f"gp{i}", [128, 2, CH], F32), Res(f"gp{i}")) for i in range(2)])

        def rms_rstd(src_ap, Rsrc):
            nt, Rn = nrm.nxt()
            S.memset("dve", nt[:, 0:1], 0.0, [], [Rn])
            S.act(junk[:], src_ap, AF.Square, [Rsrc], [R_junk, Rn], accum=nt[:, 0:1])
            S.ts("dve", nt[:, 0:1], nt[:, 0:1], 1.0 / D, RMS_EPS, ALU.mult, ALU.add, [Rn], [Rn])
            rsqrt(nt[:, 1:2], nt[:, 0:1], nt[:, 2:3], Rn, Rn, Rn)
            return nt[:, 1:2], Rn

        def mix_pre(c, step, st):
            tok0 = c * CH
            if step == 0:
                st["mt"] = mtr.nxt()
                mt, Rmt = st["mt"]
                S.dma("sp", mt[:], mixT_s[:, :, tok0:tok0 + CH].rearrange("k p n -> p k n"), r=[R_mixs], w=[Rmt], sem=Rmt)
                S.dma("sp", rdc[:], rD_s[:, :, tok0:tok0 + CH], r=[R_rDs], w=[R_rdc], sem=R_rdc)
                st["xt"] = []
                for i in range(4):
                    xt, Rx = xr.nxt()
                    S.dma("sp", xt[:], xin[tok0 + i * 128: tok0 + (i + 1) * 128, :], w=[Rx], sem=Rx)
                    st["xt"].append((xt, Rx))
                S.recip(rdc[:], rdc[:], [R_rdc], [R_rdc])
                st["x1"] = x1r.nxt()
                st["xn"] = []
                st["nt"] = nrm.nxt()
                st["nt2"] = nrm.nxt()
                S.memset("dve", st["nt"][0][:, 0:4], 0.0, [], [st["nt"][1]])
                S.memset("dve", st["nt2"][0][:, 0:4], 0.0, [], [st["nt2"][1]])
            else:
                uu = step - 1
                mt, Rmt = st["mt"]
                rb = rdc[:, uu, :]
                rb3 = bass.AP(rb.tensor, rb.offset, [list(rb.ap[0]), [0, 3], list(rb.ap[1])])
                S.tt("dve", mt[:, uu:6:2, :], mt[:, uu:6:2, :], rb3, ALU.mult, [Rmt, R_rdc], [Rmt])

        def mix_a(c, i, st):
            mt, Rmt = st["mt"]
            x1, Rx1 = st["x1"]
            nt, Rn = st["nt"]
            for half in range(2):
                tmo, R_tmo = tmor.nxt()
                for k in range(KT):
                    S.mm(tmo[:, :], mt[:, k, i * 128:(i + 1) * 128],
                         w_out_sb[:, k, half * 512:(half + 1) * 512], k == 0, k == KT - 1, [Rmt, R_wout], [R_tmo])
                S.cp("act", x1[:, i, half * 512:(half + 1) * 512], tmo[:, :], [R_tmo], [Rx1])
            S.act(junk[:], x1[:, i, :], AF.Square, [Rx1], [R_junk, Rn], accum=nt[:, i:i + 1])

        def chain(c, step, st, key):
            nt, Rn = st[key]
            if step == 0:
                S.ts("dve", nt[:, 0:4], nt[:, 0:4], 1.0 / D, RMS_EPS, ALU.mult, ALU.add, [Rn], [Rn])
                rsqrt_a(nt[:, 4:8], nt[:, 0:4], nt[:, 8:12], Rn, Rn, Rn)
            else:
                rsqrt_it(nt[:, 4:8], nt[:, 0:4], nt[:, 8:12], Rn, Rn, Rn)

        def mix_b(c, i, st):
            seg = c // 4
            x1, Rx1 = st["x1"]
            nt, Rn = st["nt"]
            xt, Rx = st["xt"][i]
            S.stt("dve", x1[:, i, :], x1[:, i, :], nt[:, 4 + i:5 + i], gg1[:, seg, :], ALU.mult, ALU.mult,
                  [Rx1, Rn, R_gg1], [Rx1])
            S.tt("dve", x1[:, i, :], x1[:, i, :], xt[:], ALU.add, [Rx1, Rx], [Rx1])

        def mix_c(c, i, st):
            x1, Rx1 = st["x1"]
            nt2, Rn2 = st["nt2"]
            S.act(junk[:], x1[:, i, :], AF.Square, [Rx1], [R_junk, Rn2], accum=nt2[:, i:i + 1])

        def mix_e(c, i, st):
            x1, Rx1 = st["x1"]
            nt2, Rn2 = st["nt2"]
            xn, Rxn = xnr.nxt()
            S.act(xn[:], x1[:, i, :], AF.Identity, [Rx1, Rn2], [Rxn], scale=nt2[:, 4 + i:5 + i])
            st["xn"].append((xn, Rxn))

        def tr_tile(c, i, st):
            seg = c // 4
            if i == 0:
                st["h2"] = h2r.nxt()
            h2T, Rh2 = st["h2"]
            xn, Rxn = st["xn"][i]
            tp, Rtp = tpr.nxt()
            for k in range(KT):
                S.tr(tp[:, k, :], xn[:, k * 128:(k + 1) * 128], ident_b[:], [Rxn, R_identb], [Rtp])
            for k in range(KT):
                S.act(h2T[:, k, i * 128:(i + 1) * 128], tp[:, k, :], AF.Identity, [Rtp, R_s2, R_sh2], [Rh2],
                      bias=sh2[:, seg, k:k + 1], scale=s2[:, seg, k:k + 1])

        def gu_tile(c, m, st):
            h2T, Rh2 = st["h2"]
            wg, Rwg = wgr.nxt()
            S.dma("sp", wg[:].rearrange("p k g c -> p (k g c)"), wgu_s[m], r=[R_wgus], w=[Rwg], sem=Rwg)
            gp, Rgp = gpr.nxt()
            for g in range(2):
                for k in range(KT):
                    S.mm(gp[:, g, :], wg[:, k, g, :], h2T[:, k, :], k == 0, k == KT - 1, [Rwg, Rh2], [Rgp])
            sg, Rsg = sgr.nxt()
            S.act(sg[:], gp[:, 0, :], AF.Silu, [Rgp], [Rsg])
            S.tt("dve", aT[:, m, :], sg[:], gp[:, 1, :], ALU.mult, [Rsg, Rgp], [R_aT])

        def dn_tile(c, i, st):
            seg = c // 4
            tok0 = c * CH
            x1, Rx1 = st["x1"]
            ms, Rms = msr.nxt()
            for half in range(2):
                tmo, R_tmo = tmor.nxt()
                for k in range(MT):
                    S.mm(tmo[:, :], aT[:, k, i * 128:(i + 1) * 128],
                         w_down_sb[:, k, half * 512:(half + 1) * 512], k == 0, k == MT - 1, [R_aT, R_wdown], [R_tmo])
                S.cp("act", ms[:, half * 512:(half + 1) * 512], tmo[:, :], [R_tmo], [Rms])
            rstd, Rn = rms_rstd(ms[:], Rms)
            S.flush_dma()
            S.stt("dve", ms[:], ms[:], rstd, gg2[:, seg, :], ALU.mult, ALU.mult, [Rms, Rn, R_gg2], [Rms])
            S.tt("dve", ms[:], ms[:], x1[:, i, :], ALU.add, [Rms, Rx1], [Rms])
            S.defer_dma(STQ, yout[tok0 + i * 128: tok0 + (i + 1) * 128, :], ms[:], r=[Rms], w=[R_y], sem=Rms)

        GUS = {}
        DNS = {}

        def at(tbl, pos, f, *args):
            tbl.setdefault(pos, []).append((f, args))

        for step in range(3):
            at(GUS, step, mix_pre, step)
        for i in range(4):
            at(GUS, 3 + 2 * i, mix_a, i)
        at(GUS, 10, chain, 0, "nt")
        at(GUS, 11, chain, 1, "nt")
        for i in range(4):
            at(GUS, 12 + i, mix_b, i)
            at(GUS, 13 + i, mix_c, i)
        at(GUS, 17, chain, 0, "nt2")
        at(GUS, 18, chain, 1, "nt2")
        for i in range(4):
            at(GUS, 19 + (i + 1) // 2, mix_e, i)
        at(GUS, 21, tr_tile, 0)
        for i in range(1, 4):
            at(DNS, i - 1, tr_tile, i)

        def run_tbl(tbl, pos, c1, st):
            for f, args in tbl.get(pos, []):
                if f is chain:
                    f(c1, args[0], st, args[1])
                else:
                    f(c1, args[0], st)

        cur = {}
        for pos in range(MT):
            run_tbl(GUS, pos, 0, cur)
        for pos in range(4):
            run_tbl(DNS, pos, 0, cur)
        for c in range(NCH):
            nxt = {}
            for m in range(MT):
                gu_tile(c, m, cur)
                if m == 1:
                    S.flush_dma()
                if c + 1 < NCH:
                    run_tbl(GUS, m, c + 1, nxt)
            for i in range(4):
                dn_tile(c, i, cur)
                if c + 1 < NCH:
                    run_tbl(DNS, i, c + 1, nxt)
            cur = nxt

    S.flush_dma()
    S.emit()
    return nc


def _alibi_slopes():
    return (2.0 ** (-8.0 * np.arange(1, 13, dtype=np.float64) / 12)).astype(np.float64)


def _build_masks(joined):
    slopes = _alibi_slopes()
    m = np.zeros((6, 128, 3, 2, 2, 128), np.float32)
    rk = np.arange(128)[:, None]
    c = np.arange(128)[None, :]
    for t in range(6):
        d = DILS[t // 2]
        for h in range(2):
            sl = slopes[2 * t + h]
            for kt in range(2):
                rel = 64 + c - 128 * kt - rk
                valid = (np.abs(rel) <= NSIDE)
                e = np.exp(-sl * d * np.abs(rel)) * valid
                same = ((kt == 0) == (c < 64)) * np.ones_like(rel)
                for ty, flag in enumerate((None, joined, 0.0)):
                    if ty == 0:
                        m[t, :, ty, h, kt, :] = e
                    else:
                        m[t, :, ty, h, kt, :] = e * np.where(same > 0, 1.0, flag)
    return m.reshape(6, 128, 3 * 2 * 2 * 128)


_PROG = {}


def kernel(x_prompt, x_sample, c_prompt, c_sample, w_ada, b_ada, g_pre_mix, w_in, w_s, b_s,
           g_gmlp, w_out, g_post_mix, g_pre_ffn, w_gu, w_down, g_post_ffn):
    debug = bool(int(os.environ.get("KDEBUG", "0")))
    stop_after = int(os.environ.get("KSTOP", "3"))
    f = lambda a: np.ascontiguousarray(np.asarray(a, dtype=np.float32))
    x_prompt, x_sample, c_prompt, c_sample = f(x_prompt), f(x_sample), f(c_prompt), f(c_sample)
    pk = lambda g: np.ascontiguousarray(f(g).reshape(KT, 128).T)
    shared = {
        "w_ada": f(w_ada)[0], "b_ada": f(b_ada)[0],
        "g_pre_mix_pk": pk(g_pre_mix[0]), "g_pre_ffn_pk": pk(g_pre_ffn[0]),
        "g_post_mix": f(g_post_mix)[0], "g_post_ffn": f(g_post_ffn)[0],
        "w_in": f(w_in)[0], "w_s": f(w_s)[0], "b_s": f(b_s)[0].reshape(512), "g_gmlp": f(g_gmlp)[0],
        "w_out": f(w_out)[0], "w_gu": f(w_gu)[0], "w_down": f(w_down)[0],
        "ident": np.eye(128, dtype=np.float32),
    }
    sel = np.zeros((NSEG, NSEG, 128), np.float32)
    for s in range(NSEG):
        sel[s, s, :] = 1.0
    shared["sel"] = sel
    masks = {True: _build_masks(1.0), False: _build_masks(0.0)}
    in_maps = []
    for core in range(NCORES):
        if core < 4:
            xs = [x_prompt[core, :SEG], x_prompt[core, SEG:], x_sample[core]]
            cs = [c_prompt[core], c_prompt[core], c_sample[core]]
            joined = True
        else:
            b0 = 4 + 3 * (core - 4)
            xs = [x_sample[b0], x_sample[b0 + 1], x_sample[b0 + 2]]
            cs = [c_sample[b0], c_sample[b0 + 1], c_sample[b0 + 2]]
            joined = False
        m = dict(shared)
        m["xc"] = np.ascontiguousarray(np.concatenate(xs, axis=0))
        cc = np.stack(cs, axis=0)
        m["ct"] = np.ascontiguousarray(cc.reshape(NSEG, KT, 128).transpose(2, 1, 0))
        m["emask"] = masks[joined]
        in_maps.append(m)

    key = (debug, stop_after)
    nc = build_program(debug=debug, stop_after=stop_after)
    ncores_run = int(os.environ.get("KCORES", NCORES))
    res = run_bass_kernel_spmd(nc, in_maps[:ncores_run], core_ids=list(range(ncores_run)), trace=bool(int(os.environ.get('KTRACE', '0'))))
    if os.environ.get('KTRACE'):
        print('EXEC_TIME_NS', res.exec_time_ns)
    if ncores_run < NCORES:
        kernel.last_results = res.results
        return None
    if debug:
        kernel.last_results = res.results
    y_prompt = np.zeros((4, 2 * SEG, D), np.float32)
    y_sample = np.zeros((16, SEG, D), np.float32)
    for core in range(NCORES):
        y = np.asarray(res.results[core]["y"], dtype=np.float32)
        if core < 4:
            y_prompt[core] = y[:2 * SEG]
            y_sample[core] = y[2 * SEG:]
        else:
            b0 = 4 + 3 * (core - 4)
            for s in range(3):
                y_sample[b0 + s] = y[s * SEG:(s + 1) * SEG]
    return (y_prompt, y_sample)
```

```python
import os
from contextlib import ExitStack

import numpy as np
import concourse.bass as bass
import concourse.mybir as mybir
from concourse.bass_utils import run_bass_kernel_spmd

F32 = mybir.dt.float32
BF16 = mybir.dt.bfloat16
I32 = mybir.dt.int32
AF = mybir.ActivationFunctionType
ALU = mybir.AluOpType
AX = mybir.AxisListType

NCORES = 8
NT = 6144
SEG = 2048
NSEG = 3
D = 1024
KT = 8
CH = 512
NCH = int(os.environ.get('KNCH', NT // CH))
DFF = 2816
MT = DFF // 128
DILS = (1, 4, 16)
NSIDE = 64
RMS_EPS = 1e-6
LN_EPS = 1e-5
MAGIC = 0x5F3759DF


class Res:
    def __init__(self, name, multi=False):
        self.name = name
        self.multi = multi
        self.last_write = None
        self.rd_eng = {}
        self.rd_dma = []
        self.writes_all = []
        self.sem = None
        self.dma_count = 0


class Op:
    __slots__ = ("eng", "fn", "deps", "is_dma", "res", "sem_count", "count", "signal")

    def __init__(self, eng, fn, is_dma=False):
        self.eng = eng
        self.fn = fn
        self.deps = []
        self.is_dma = is_dma
        self.res = None
        self.sem_count = 0
        self.count = 0
        self.signal = False


ENGS = ("sp", "act", "pool", "pe", "dve")
STQ = os.environ.get("KSTQ", "act")


class Sched:
    def __init__(self, nc, es):
        self.nc = nc
        self.es = es
        self.q = {e: [] for e in ENGS}
        self.dma_res = []
        self.pending = {e: [] for e in ENGS}
        self.last_op = {e: None for e in ENGS}
        self.dma_since_barrier = []
        self.deferred = []

    def _track(self, o, reads, writes):
        deps = []
        for r in reads:
            if r.multi:
                deps.extend(r.writes_all)
            elif r.last_write is not None:
                deps.append(r.last_write)
        for w in writes:
            if w.multi:
                continue
            if w.last_write is not None:
                deps.append(w.last_write)
            deps.extend(w.rd_eng.values())
            deps.extend(w.rd_dma)
        for r in reads:
            if r.multi:
                continue
            if o.is_dma:
                r.rd_dma.append(o)
            else:
                r.rd_eng[o.eng] = o
        for w in writes:
            if w.multi:
                w.writes_all.append(o)
            else:
                w.last_write = o
                w.rd_eng = {}
                w.rd_dma = []
        if self.pending[o.eng]:
            deps.extend(self.pending[o.eng])
            self.pending[o.eng] = []
        seen = set()
        out = []
        for d in deps:
            if d is o or id(d) in seen:
                continue
            seen.add(id(d))
            out.append(d)
        o.deps = out
        self.q[o.eng].append(o)
        if not o.is_dma:
            self.last_op[o.eng] = o

    def op(self, eng, fn, r=(), w=()):
        o = Op(eng, fn)
        self._track(o, r, w)
        return o

    def dma(self, eng, out, in_, r=(), w=(), sem=None):
        o = Op(eng, lambda e: e.dma_start(out=out, in_=in_), is_dma=True)
        if sem.sem is None:
            sem.sem = self.es.enter_context(self.nc.semaphore("d_" + sem.name))
            self.dma_res.append(sem)
        sem.dma_count += 16
        o.res = sem
        o.sem_count = sem.dma_count
        self._track(o, r, w)
        self.dma_since_barrier.append(o)
        return o

    def defer_dma(self, *a, **k):
        self.deferred.append((a, k))

    def flush_dma(self):
        for a, k in self.deferred:
            self.dma(*a, **k)
        self.deferred = []

    def barrier(self):
        self.flush_dma()
        deps = [o for o in self.last_op.values() if o is not None] + list(self.dma_since_barrier)
        self.dma_since_barrier = []
        for e in ENGS:
            self.pending[e] = list(self.pending[e]) + deps

    def mm(self, out, lhsT, rhs, start, stop, r, w):
        return self.op("pe", lambda e: e.matmul(out, lhsT=lhsT, rhs=rhs, start=start, stop=stop), r, w)

    def tr(self, out, in_, ident, r, w):
        return self.op("pe", lambda e: e.transpose(out=out, in_=in_, identity=ident), r, w)

    def act(self, out, in_, func, r, w, bias=None, scale=None, accum=None):
        kw = {}
        if bias is not None:
            kw["bias"] = bias
        if scale is not None:
            kw["scale"] = scale
        if accum is not None:
            kw["accum_out"] = accum
        return self.op("act", lambda e: e.activation(out=out, in_=in_, func=func, **kw), r, w)

    def tt(self, eng, out, in0, in1, op, r, w):
        return self.op(eng, lambda e: e.tensor_tensor(out=out, in0=in0, in1=in1, op=op), r, w)

    def ts(self, eng, out, in0, s1, s2, op0, op1, r, w):
        if op1 is None:
            return self.op(eng, lambda e: e.tensor_scalar(out=out, in0=in0, scalar1=s1, scalar2=None, op0=op0), r, w)
        return self.op(eng, lambda e: e.tensor_scalar(out=out, in0=in0, scalar1=s1, scalar2=s2, op0=op0, op1=op1), r, w)

    def stt(self, eng, out, in0, scalar, in1, op0, op1, r, w):
        return self.op(eng, lambda e: e.scalar_tensor_tensor(out=out, in0=in0, scalar=scalar, in1=in1, op0=op0, op1=op1), r, w)

    def red(self, out, in_, r, w):
        return self.op("dve", lambda e: e.tensor_reduce(out=out, in_=in_, axis=AX.X, op=ALU.add), r, w)

    def cp(self, eng, out, in_, r, w):
        if eng == "act":
            return self.op("act", lambda e: e.activation(out=out, in_=in_, func=AF.Copy), r, w)
        return self.op(eng, lambda e: e.tensor_copy(out=out, in_=in_), r, w)

    def recip(self, out, in_, r, w):
        return self.op("dve", lambda e: e.reciprocal(out=out, in_=in_), r, w)

    def memset(self, eng, ap, val, r, w):
        return self.op(eng, lambda e: e.memset(ap, val), r, w)

    def emit(self):
        nc = self.nc
        for e in ENGS:
            for o in self.q[e]:
                for d in o.deps:
                    if not d.is_dma and not (d.eng == e and e == "pe"):
                        d.signal = True
        for e in ENGS:
            c = 0
            for o in self.q[e]:
                if not o.is_dma and o.signal:
                    c += 1
                    o.count = c
        sems = {e: self.es.enter_context(nc.semaphore("s_" + e)) for e in ENGS}
        dma_res = self.dma_res
        q = self.q

        def run(e, eng, final=False):
            waited = {}
            for o in q[e]:
                need = {}
                for d in o.deps:
                    if d.is_dma:
                        s, v = d.res.sem, d.sem_count
                    else:
                        if d.eng == e and e == "pe":
                            continue
                        s, v = sems[d.eng], d.count
                    k = id(s)
                    if k not in need or need[k][1] < v:
                        need[k] = (s, v)
                for k, (s, v) in need.items():
                    if waited.get(k, 0) < v:
                        eng.wait_ge(s, v)
                        waited[k] = v
                ins = o.fn(eng)
                if o.is_dma:
                    ins.then_inc(o.res.sem, 16)
                elif o.signal:
                    ins.then_inc(sems[e], 1)
            if final:
                for rr in dma_res:
                    if waited.get(id(rr.sem), 0) < rr.dma_count:
                        eng.wait_ge(rr.sem, rr.dma_count)

        with nc.Block() as block:
            @block.sync
            def _(eng):
                run("sp", eng, final=True)

            @block.scalar
            def _(eng):
                run("act", eng)

            @block.gpsimd
            def _(eng):
                run("pool", eng)

            @block.tensor
            def _(eng):
                run("pe", eng)

            @block.vector
            def _(eng):
                run("dve", eng)


class Ring:
    def __init__(self, items):
        self.items = items
        self.i = 0

    def nxt(self):
        it = self.items[self.i % len(self.items)]
        self.i += 1
        return it


def bc_last(ap, n):
    return bass.AP(ap.tensor, ap.offset, [list(x) for x in ap.ap] + [[0, n]])


def build_program(debug=False, stop_after=3):
    nc = bass.Bass("TRN2", target_bir_lowering=False)
    es = ExitStack()
    S = Sched(nc, es)

    def din(name, shape, dt=F32):
        return nc.dram_tensor(name, list(shape), dt, kind="ExternalInput").ap()

    scratch_kind = "ExternalOutput" if debug else "Internal"

    xin = din("xc", [NT, D])
    ctin = din("ct", [128, KT, NSEG])
    w_ada = din("w_ada", [D, 6 * D])
    b_ada = din("b_ada", [6 * D])
    gpm_pk = din("g_pre_mix_pk", [128, KT])
    gpf_pk = din("g_pre_ffn_pk", [128, KT])
    g_post_mix = din("g_post_mix", [D])
    g_post_ffn = din("g_post_ffn", [D])
    w_in = din("w_in", [D, 2816])
    w_s = din("w_s", [4, 128, 128])
    b_s = din("b_s", [512])
    g_gmlp = din("g_gmlp", [256])
    w_out = din("w_out", [D, D])
    w_gu = din("w_gu", [D, 2 * DFF])
    w_down = din("w_down", [DFF, D])
    identin = din("ident", [128, 128])
    selin = din("sel", [NSEG, NSEG, 128])
    emask = din("emask", [6, 128, 3 * 2 * 2 * 128])

    yout = nc.dram_tensor("y", [NT, D], F32, kind="ExternalOutput").ap()
    qT_s = nc.dram_tensor("qT_s", [6, 128, NT], BF16, kind=scratch_kind).ap()
    kT_s = nc.dram_tensor("kT_s", [6, 128, NT], BF16, kind=scratch_kind).ap()
    V_s = nc.dram_tensor("V_s", [NT, 768], BF16, kind=scratch_kind).ap()
    mixT_s = nc.dram_tensor("mixT_s", [8, 128, NT], BF16, kind=scratch_kind).ap()
    wgu_s = nc.dram_tensor("wgu_s", [MT, 128, KT * 2 * 128], BF16, kind="Internal").ap()
    rD_s = nc.dram_tensor("rD_s", [128, 2, NT], F32, kind="Internal").ap()

    R_qs = Res("qs", multi=True)
    R_ks = Res("ks", multi=True)
    R_vs = Res("vs", multi=True)
    R_mixs = Res("mixs", multi=True)
    R_wgus = Res("wgus", multi=True)
    R_y = Res("y", multi=True)
    R_rDs = Res("rDs", multi=True)

    def sbuf(stack, name, shape, dt):
        return stack.enter_context(nc.sbuf_tensor("sb_" + name, list(shape), dt))

    def psum(stack, name, shape, dt):
        return stack.enter_context(nc.psum_tensor("pp_" + name, list(shape), dt))

    ident_f = sbuf(es, "ident_f", [128, 128], F32)
    ident_b = sbuf(es, "ident_b", [128, 128], BF16)
    ones_b = sbuf(es, "ones_b", [128, 128], BF16)
    s1 = sbuf(es, "s1", [128, NSEG, KT], F32)
    sh1 = sbuf(es, "sh1", [128, NSEG, KT], F32)
    s2 = sbuf(es, "s2", [128, NSEG, KT], F32)
    sh2 = sbuf(es, "sh2", [128, NSEG, KT], F32)
    gg1 = sbuf(es, "gg1", [128, NSEG, D], BF16)
    gg2 = sbuf(es, "gg2", [128, NSEG, D], BF16)
    R_identf, R_identb, R_ones = Res("identf"), Res("identb"), Res("ones")
    R_s1, R_sh1, R_s2, R_sh2, R_gg1, R_gg2 = (Res(n) for n in ("s1", "sh1", "s2", "sh2", "gg1", "gg2"))

    S.dma("sp", ident_f[:], identin[:, :], w=[R_identf], sem=R_identf)
    S.dma("pool", ident_b[:], identin[:, :], w=[R_identb], sem=R_identb)
    S.memset("dve", ones_b[:], 1.0, [], [R_ones])

    def rsqrt_a(y, m, t, Ry, Rm, Rt):
        S.ts("dve", y.bitcast(I32), m.bitcast(I32), 1, None, ALU.arith_shift_right, None, [Rm], [Ry])
        S.ts("dve", y.bitcast(I32), y.bitcast(I32), -1, MAGIC, ALU.mult, ALU.add, [Ry], [Ry])
        rsqrt_it(y, m, t, Ry, Rm, Rt)

    def rsqrt_it(y, m, t, Ry, Rm, Rt):
        S.tt("dve", t, y, y, ALU.mult, [Ry], [Rt])
        S.tt("dve", t, t, m, ALU.mult, [Rt, Rm], [Rt])
        S.ts("dve", t, t, -0.5, 1.5, ALU.mult, ALU.add, [Rt], [Rt])
        S.tt("dve", y, y, t, ALU.mult, [Ry, Rt], [Ry])

    def rsqrt(y, m, t, Ry, Rm, Rt):
        S.ts("dve", y.bitcast(I32), m.bitcast(I32), 1, None, ALU.arith_shift_right, None, [Rm], [Ry])
        S.ts("dve", y.bitcast(I32), y.bitcast(I32), -1, MAGIC, ALU.mult, ALU.add, [Ry], [Ry])
        for _ in range(2):
            S.tt("dve", t, y, y, ALU.mult, [Ry], [Rt])
            S.tt("dve", t, t, m, ALU.mult, [Rt, Rm], [Rt])
            S.ts("dve", t, t, -0.5, 1.5, ALU.mult, ALU.add, [Rt], [Rt])
            S.tt("dve", y, y, t, ALU.mult, [Ry, Rt], [Ry])

    p1w = ExitStack()
    w_in_sb = sbuf(p1w, "w_in_sb", [128, KT, 2816], BF16)
    R_win = Res("w_in")
    w_in_v = w_in.rearrange("(k p) n -> p k n", p=128)
    for (d0, s0, n) in ((0, 0, 768), (768, 768, 768), (1536, 2304, 256), (1792, 1536, 768), (2560, 2560, 256)):
        for kh in range(2):
            S.dma("pool", w_in_sb[:, kh * 4:(kh + 1) * 4, d0:d0 + n], w_in_v[:, kh * 4:(kh + 1) * 4, s0:s0 + n],
                  w=[R_win], sem=R_win)

    with ExitStack() as ps_:
        ct = sbuf(ps_, "ct", [128, KT, NSEG], F32)
        mod_sb = sbuf(ps_, "mod_sb", [NSEG, 6 * D], F32)
        bada = sbuf(ps_, "bada", [NSEG, 6 * D], F32)
        sel_sb = sbuf(ps_, "sel_sb", [NSEG, NSEG, 128], F32)
        gpm_sb = sbuf(ps_, "gpm_sb", [128, KT], F32)
        gpf_sb = sbuf(ps_, "gpf_sb", [128, KT], F32)
        gpost = [sbuf(ps_, "gpostm", [128, D], F32), sbuf(ps_, "gpostf", [128, D], F32)]
        tmp3 = sbuf(ps_, "tmp3", [128, KT, NSEG], F32)
        wa = [sbuf(ps_, f"wa{i}", [128, KT, 512], F32) for i in range(2)]
        modps = [psum(ps_, f"modps{i}", [128, 512], F32) for i in range(2)]
        modT = psum(ps_, "modT", [128, 4, KT, NSEG], F32)
        R_ct, R_mod, R_bada, R_sel, R_gpm, R_gpf = (Res(n) for n in ("ct", "mod", "bada", "sel", "gpm", "gpf"))
        R_gpost = [Res("gpostm"), Res("gpostf")]
        R_tmp3 = Res("tmp3")
        wa_ring = Ring([(wa[i], Res(f"wa{i}")) for i in range(2)])
        mp_ring = Ring([(modps[i], Res(f"modps{i}")) for i in range(2)])
        R_modT = Res("modT")

        S.dma("sp", ct[:], ctin[:, :, :], w=[R_ct], sem=R_ct)
        S.dma("sp", bada[:], b_ada.partition_broadcast(NSEG), w=[R_bada], sem=R_bada)
        S.dma("sp", sel_sb[:], selin[:, :, :], w=[R_sel], sem=R_sel)
        S.dma("sp", gpm_sb[:], gpm_pk[:, :], w=[R_gpm], sem=R_gpm)
        S.dma("sp", gpf_sb[:], gpf_pk[:, :], w=[R_gpf], sem=R_gpf)
        S.dma("sp", gpost[0][:], g_post_mix.partition_broadcast(128), w=[R_gpost[0]], sem=R_gpost[0])
        S.dma("sp", gpost[1][:], g_post_ffn.partition_broadcast(128), w=[R_gpost[1]], sem=R_gpost[1])
        S.act(ct[:], ct[:], AF.Silu, [R_ct], [R_ct])
        w_ada_v = w_ada.rearrange("(k p) n -> p k n", p=128)
        for cb in range(12):
            wt, Rw = wa_ring.nxt()
            S.dma("sp", wt[:], w_ada_v[:, :, cb * 512:(cb + 1) * 512], w=[Rw], sem=Rw)
            mp, Rmp = mp_ring.nxt()
            for k in range(KT):
                S.mm(mp[0:NSEG, :], ct[:, k, :], wt[:, k, :], k == 0, k == KT - 1, [R_ct, Rw], [Rmp])
            S.tt("dve", mod_sb[:, cb * 512:(cb + 1) * 512], mp[0:NSEG, :], bada[:, cb * 512:(cb + 1) * 512],
                 ALU.add, [Rmp, R_bada], [R_mod])
        for wi, j in enumerate((0, 1, 3, 4)):
            for k in range(KT):
                S.mm(modT[:, wi, k, :], mod_sb[0:NSEG, j * D + k * 128: j * D + (k + 1) * 128],
                     ident_f[0:NSEG, 0:NSEG], True, True, [R_mod, R_identf], [R_modT])
        for (wsh, wsc, sh_t, s_t, g_sb, Rsh, Rs, Rg) in ((0, 1, sh1, s1, gpm_sb, R_sh1, R_s1, R_gpm),
                                                        (2, 3, sh2, s2, gpf_sb, R_sh2, R_s2, R_gpf)):
            S.cp("dve", sh_t[:].rearrange("p s k -> p k s"), modT[:, wsh, :, :], [R_modT], [Rsh])
            S.ts("dve", tmp3[:], modT[:, wsc, :, :], 1.0, None, ALU.add, None, [R_modT], [R_tmp3])
            S.tt("dve", s_t[:].rearrange("p s k -> p k s"), tmp3[:], bc_last(g_sb[:, :], NSEG), ALU.mult,
                 [R_tmp3, Rg], [Rs])
        for s in range(NSEG):
            for gi, (j, gg_t, Rgg) in enumerate(((2, gg1, R_gg1), (5, gg2, R_gg2))):
                for half in range(2):
                    mp, Rmp = mp_ring.nxt()
                    S.mm(mp[:, :], sel_sb[0:NSEG, s, :], mod_sb[0:NSEG, j * D + half * 512: j * D + (half + 1) * 512],
                         True, True, [R_sel, R_mod], [Rmp])
                    S.tt("dve", gg_t[:, s, half * 512:(half + 1) * 512], mp[:, :],
                         gpost[gi][:, half * 512:(half + 1) * 512], ALU.mult, [Rmp, R_gpost[gi]], [Rgg])
        S.barrier()

    wgu_src = w_gu.rearrange("(k p) (g m c) -> m p k g c", p=128, g=2, c=128)
    R_wgu_cvt = Res("wgucvt")
    for m in range(0 if not os.environ.get('KNOWGU') else MT, MT):
        dst = wgu_s[m].rearrange("p (k g c) -> p k g c", k=KT, g=2)
        for g in range(2):
            S.dma("pool", dst[:, :, g, :], wgu_src[m][:, :, g, :], w=[R_wgus], sem=R_wgu_cvt)

    if stop_after < 1:
        S.flush_dma()
        S.emit()
        return nc
    with ExitStack() as p1:
        ws_f = sbuf(p1, "ws_f", [128, 4, 128], F32)
        w_sT = sbuf(p1, "w_sT", [128, 4, 128], BF16)
        bs_f = sbuf(p1, "bs_f", [1, 512], F32)
        bs_hi = sbuf(p1, "bs_hi", [1, 512], BF16)
        bs_hf = sbuf(p1, "bs_hf", [1, 512], F32)
        bs_lo = sbuf(p1, "bs_lo", [1, 512], BF16)
        ggm = sbuf(p1, "ggm", [128, 256], F32)
        R_wsf, R_wsT, R_bsf, R_bshi, R_bshf, R_bslo, R_ggm = (Res(n) for n in
                                                              ("wsf", "wsT", "bsf", "bshi", "bshf", "bslo", "ggm"))
        S.dma("sp", ws_f[:], w_s.rearrange("g t s -> t g s"), w=[R_wsf], sem=R_wsf)
        S.dma("sp", bs_f[:], b_s.partition_broadcast(1), w=[R_bsf], sem=R_bsf)
        S.dma("sp", ggm[:], g_gmlp.partition_broadcast(128), w=[R_ggm], sem=R_ggm)
        S.cp("dve", bs_hi[:], bs_f[:], [R_bsf], [R_bshi])
        S.cp("dve", bs_hf[:], bs_hi[:], [R_bshi], [R_bshf])
        S.tt("dve", bs_lo[:], bs_f[:], bs_hf[:], ALU.subtract, [R_bsf, R_bshf], [R_bslo])

        xr = Ring([(sbuf(p1, f"x{i}", [128, D], F32), Res(f"x{i}")) for i in range(8)])
        xnr = Ring([(sbuf(p1, f"xn{i}", [128, D], BF16), Res(f"xn{i}")) for i in range(8)])
        junk = sbuf(p1, "junk", [128, D], BF16)
        R_junk = Res("junk")
        ssr = Ring([(sbuf(p1, f"ss{i}", [128, 4], F32), Res(f"ss{i}")) for i in range(2)])
        rsr = Ring([(sbuf(p1, f"rs{i}", [128, 4], F32), Res(f"rs{i}")) for i in range(2)])
        rtr = Ring([(sbuf(p1, f"rt{i}", [128, 4], F32), Res(f"rt{i}")) for i in range(2)])
        hTr = Ring([(sbuf(p1, f"hT{i}", [128, KT, CH], BF16), Res(f"hT{i}")) for i in range(2)])
        qkr = Ring([(sbuf(p1, f"qk{i}", [128, 12, CH], BF16), Res(f"qk{i}")) for i in range(2)])
        vstr = Ring([(sbuf(p1, f"vst{i}", [128, 768], BF16), Res(f"vst{i}")) for i in range(4)])
        gur = Ring([(sbuf(p1, f"gu{i}", [128, 2, CH], F32), Res(f"gu{i}")) for i in range(2)])
        gmr = Ring([(sbuf(p1, f"gm{i}", [128, 2, CH], BF16), Res(f"gm{i}")) for i in range(2)])
        gvr = Ring([(sbuf(p1, f"gv{i}", [128, 256], F32), Res(f"gv{i}")) for i in range(8)])
        sqr = Ring([(sbuf(p1, f"sq{i}", [128, 256], F32), Res(f"sq{i}")) for i in range(8)])
        vnr = Ring([(sbuf(p1, f"vn{i}", [128, 256], BF16), Res(f"vn{i}")) for i in range(8)])
        lnr = Ring([(sbuf(p1, f"ln{i}", [128, 5, 16], F32), Res(f"ln{i}")) for i in range(2)])
        hfr = Ring([(sbuf(p1, f"hf{i}", [128, KT, 128], F32), Res(f"hf{i}")) for i in range(2)])
        tpr = Ring([(psum(p1, f"tp{i}", [128, KT, 128], BF16), Res(f"tp{i}")) for i in range(2)])
        fmr = Ring([(psum(p1, f"fm{i}", [128, CH], F32), Res(f"fm{i}")) for i in range(2)])
        tmr = Ring([(psum(p1, f"tm{i}", [128, D], F32), Res(f"tm{i}")) for i in range(2)])

        for g in range(4):
            fm, Rfm = fmr.nxt()
            S.mm(fm[:, 0:128], ws_f[:, g, :], ident_f[:, :], True, True, [R_wsf, R_identf], [Rfm])
            S.cp("dve", w_sT[:, g, :], fm[:, 0:128], [Rfm], [R_wsT])

        def stageA1(c):
            tok0 = c * CH
            ss, Rss = ssr.nxt()
            rs, Rrs = rsr.nxt()
            rt, Rrt = rtr.nxt()
            S.memset("dve", ss[:], 0.0, [], [Rss])
            xs = []
            for i in range(4):
                xt, Rx = xr.nxt()
                S.dma("sp", xt[:], xin[tok0 + i * 128: tok0 + (i + 1) * 128, :], w=[Rx], sem=Rx)
                S.act(junk[:], xt[:], AF.Square, [Rx], [R_junk, Rss], accum=ss[:, i:i + 1])
                xs.append((xt, Rx))
            S.ts("dve", ss[:], ss[:], 1.0 / D, RMS_EPS, ALU.mult, ALU.add, [Rss], [Rss])
            rsqrt(rs[:], ss[:], rt[:], Rrs, Rss, Rrt)
            return (xs, rs, Rrs)

        def stageA1b(pre):
            xs, rs, Rrs = pre
            xns = []
            for i in range(4):
                xt, Rx = xs[i]
                xn, Rxn = xnr.nxt()
                S.act(xn[:], xt[:], AF.Identity, [Rx, Rrs], [Rxn], scale=rs[:, i:i + 1])
                xns.append((xn, Rxn))
            return xns

        def stageA2(c, xns, tiles=(0, 1, 2, 3), h=None):
            seg = c // 4
            hT, RhT = h if h is not None else hTr.nxt()
            for i in tiles:
                xn, Rxn = xns[i]
                tp, Rtp = tpr.nxt()
                for k in range(KT):
                    S.tr(tp[:, k, :], xn[:, k * 128:(k + 1) * 128], ident_b[:], [Rxn, R_identb], [Rtp])
                ho = hT[:, :, i * 128:(i + 1) * 128]
                hf, Rhf = hfr.nxt()
                S.tt("dve", hf[:], tp[:, :, :], bc_last(s1[:, seg, :], 128), ALU.mult, [Rtp, R_s1], [Rhf])
                S.tt("dve", ho, hf[:], bc_last(sh1[:, seg, :], 128), ALU.add, [Rhf, R_sh1], [RhT])
            return hT, RhT

        def fm_tile(c, m, hT, RhT, st):
            qk, Rqk = st["qk"]
            gu, Rgu = st["gu"]
            fm, Rfm = fmr.nxt()
            for k in range(KT):
                S.mm(fm[:, :], w_in_sb[:, k, m * 128:(m + 1) * 128], hT[:, k, :], k == 0, k == KT - 1,
                     [R_win, RhT], [Rfm])
            if m < 12:
                S.cp("act" if m % 2 == 0 else "dve", qk[:, m, :], fm[:, :], [Rfm], [Rqk])
            else:
                S.act(gu[:, m - 12, :], fm[:, :], AF.Gelu_apprx_tanh, [Rfm], [Rgu])
            if m == 11:
                tok0 = c * CH
                S.defer_dma(STQ, qT_s[:, :, tok0:tok0 + CH].rearrange("t p n -> p t n"), qk[:, 0:6, :],
                            r=[Rqk], w=[R_qs], sem=Rqk)
                S.defer_dma(STQ, kT_s[:, :, tok0:tok0 + CH].rearrange("t p n -> p t n"), qk[:, 6:12, :],
                            r=[Rqk], w=[R_ks], sem=Rqk)

        def tm_tile(c, i, hT, RhT, st):
            tok0 = c * CH
            if i == 0:
                st["gm"] = gmr.nxt()
                st["vn"] = {}
            tm, Rtm = tmr.nxt()
            for half in range(2):
                for k in range(KT):
                    S.mm(tm[:, half * 512:(half + 1) * 512], hT[:, k, i * 128:(i + 1) * 128],
                         w_in_sb[:, k, 1792 + half * 512: 1792 + (half + 1) * 512], k == 0, k == KT - 1,
                         [R_win, RhT], [Rtm])
            vst, Rvst = vstr.nxt()
            S.cp("act", vst[:], tm[:, 0:768], [Rtm], [Rvst])
            S.defer_dma(STQ, V_s[tok0 + i * 128: tok0 + (i + 1) * 128, :], vst[:], r=[Rvst], w=[R_vs], sem=Rvst)
            if i == 0:
                st["ln"] = lnr.nxt()
                st["gv"] = {}
            ln, Rln = st["ln"]
            gv, Rgv = gvr.nxt()
            sq, Rsq = sqr.nxt()
            S.act(gv[:], tm[:, 768:1024], AF.Gelu_apprx_tanh, [Rtm, Rvst], [Rgv])
            gv3 = gv[:].rearrange("p (g c) -> p g c", g=4)
            sq3 = sq[:].rearrange("p (g c) -> p g c", g=4)
            S.act(sq[:], gv[:], AF.Square, [Rgv], [Rsq])
            S.red(ln[:, 0, i * 4:(i + 1) * 4], gv3, [Rgv], [Rln])
            S.red(ln[:, 1, i * 4:(i + 1) * 4], sq3, [Rsq], [Rln])
            st["gv"][i] = (gv, Rgv, sq, Rsq)

        def ln_chain(c, st):
            ln, Rln = st["ln"]
            S.ts("dve", ln[:, 0, :], ln[:, 0, :], 1.0 / 64, None, ALU.mult, None, [Rln], [Rln])
            S.tt("dve", ln[:, 2, :], ln[:, 0, :], ln[:, 0, :], ALU.mult, [Rln], [Rln])
            S.stt("dve", ln[:, 1, :], ln[:, 1, :], 1.0 / 64, ln[:, 2, :], ALU.mult, ALU.subtract, [Rln], [Rln])
            S.ts("dve", ln[:, 1, :], ln[:, 1, :], LN_EPS, None, ALU.add, None, [Rln], [Rln])
            rsqrt(ln[:, 3, :], ln[:, 1, :], ln[:, 4, :], Rln, Rln, Rln)

        def vn_tile(c, i, st):
            ln, Rln = st["ln"]
            gv, Rgv, sq, Rsq = st["gv"][i]
            vn, Rvn = vnr.nxt()
            gv3 = gv[:].rearrange("p (g c) -> p g c", g=4)
            sq3 = sq[:].rearrange("p (g c) -> p g c", g=4)
            S.tt("dve", sq3, gv3, bc_last(ln[:, 0, i * 4:(i + 1) * 4], 64), ALU.subtract, [Rgv, Rln], [Rsq])
            S.tt("dve", sq3, sq3, bc_last(ln[:, 3, i * 4:(i + 1) * 4], 64), ALU.mult, [Rsq, Rln], [Rsq])
            S.tt("dve", vn[:], sq[:], ggm[:], ALU.mult, [Rsq, R_ggm], [Rvn])
            st["vn"][i] = (vn, Rvn)

        def gm_tile(c, i, st):
            tok0 = c * CH
            gm, Rgm = st["gm"]
            gu, Rgu = st["gu"]
            vn, Rvn = st["vn"][i]
            fm, Rfm = fmr.nxt()
            for g in range(4):
                o = fm[(g % 2) * 64:(g % 2) * 64 + 64, (g // 2) * 128:(g // 2) * 128 + 128]
                S.mm(o, vn[:, g * 64:(g + 1) * 64], w_sT[:, g, :], True, False, [Rvn, R_wsT], [Rfm])
                S.mm(o, ones_b[0:1, 0:64], bs_hi[0:1, g * 128:(g + 1) * 128], False, False, [R_ones, R_bshi], [Rfm])
                S.mm(o, ones_b[0:1, 0:64], bs_lo[0:1, g * 128:(g + 1) * 128], False, True, [R_ones, R_bslo], [Rfm])
            S.tt("dve", gm[:, :, i * 128:(i + 1) * 128], gu[:, :, i * 128:(i + 1) * 128],
                 fm[:, 0:256].rearrange("p (a t) -> p a t", a=2), ALU.mult, [Rgu, Rfm], [Rgm])
            if i == 3:
                S.defer_dma(STQ, mixT_s[6:8, :, tok0:tok0 + CH].rearrange("t p n -> p t n"), gm[:, :, :],
                            r=[Rgm], w=[R_mixs], sem=Rgm)

        hcur = stageA2(0, stageA1b(stageA1(0)))
        for c in range(NCH):
            st = {}
            pre_next = stageA1(c + 1) if c + 1 < NCH else None
            xns_next = None
            hT, RhT = hcur
            order = [("tm", 0), ("tm", 1), ("fm", 12), ("tm", 2), ("fm", 13), ("tm", 3), ("A1b", 0), ("fm", 0), ("ln", 0),
                     ("fm", 1), ("fm", 2), ("vn", 0), ("vn", 1), ("fm", 3), ("fm", 4), ("vn", 2), ("vn", 3),
                     ("fm", 5), ("gm", 0), ("A2", 0), ("fm", 6), ("gm", 1), ("fm", 7), ("A2", 1), ("gm", 2),
                     ("fm", 8), ("gm", 3), ("fm", 9), ("A2", 2), ("fm", 10), ("fm", 11), ("A2", 3)]
            hnext = None
            st["qk"] = qkr.nxt()
            st["gu"] = gur.nxt()
            for kind, idx in order:
                if kind == "fm":
                    fm_tile(c, idx, hT, RhT, st)
                elif kind == "tm":
                    tm_tile(c, idx, hT, RhT, st)
                elif kind == "gm":
                    gm_tile(c, idx, st)
                elif kind == "ln":
                    ln_chain(c, st)
                elif kind == "A1b":
                    xns_next = stageA1b(pre_next) if c + 1 < NCH else None
                elif kind == "vn":
                    vn_tile(c, idx, st)
                else:
                    if c + 1 < NCH:
                        hnext = stageA2(c + 1, xns_next, tiles=(idx,), h=hnext)
                    if idx == 3:
                        S.flush_dma()
            hcur = hnext
        S.barrier()

    p1w.close()
    p3w = ExitStack()
    w_out_sb = sbuf(p3w, "w_out_sb", [128, KT, D], BF16)
    R_wout = Res("wout")
    w_out_v = w_out.rearrange("(k p) n -> p k n", p=128)
    for kh in range(2):
        S.dma("pool", w_out_sb[:, kh * 4:(kh + 1) * 4, :], w_out_v[:, kh * 4:(kh + 1) * 4, :], w=[R_wout], sem=R_wout)
    if stop_after >= 2:
      with ExitStack() as p2:
        qAb = [sbuf(p2, f"qA{i}", [128, NT], BF16) for i in range(2)]
        qBb = [sbuf(p2, f"qB{i}", [128, NT], BF16) for i in range(2)]
        kTb = [sbuf(p2, f"kTb{i}", [128, NT], BF16) for i in range(2)]
        Vcb = [sbuf(p2, f"Vcb{i}", [128, 48, 128], BF16) for i in range(2)]
        mkb = [sbuf(p2, f"mkb{i}", [128, 3, 512], F32) for i in range(2)]
        R_qAb = [Res(f"qA{i}") for i in range(2)]
        R_qBb = [Res(f"qB{i}") for i in range(2)]
        R_kTb = [Res(f"kTb{i}") for i in range(2)]
        R_Vcb = [Res(f"Vcb{i}") for i in range(2)]
        R_mkb = [Res(f"mkb{i}") for i in range(2)]
        R_qAz = [Res(f"qAz{i}") for i in range(2)]
        R_qBz = [Res(f"qBz{i}") for i in range(2)]
        for i in range(2):
            S.memset("dve", qAb[i][64:128, :], 0.0, [], [R_qAz[i]])
            S.op("act", lambda e, ap=qBb[i][0:64, :]: e.memzero(ap), [], [R_qBz[i]])
        numTb = [sbuf(p2, f"numT{i}", [128, NT], BF16) for i in range(2)]
        R_numTb = [Res(f"numT{i}") for i in range(2)]
        Dt = sbuf(p2, "Dt", [128, NT], F32)
        R_D = Res("D")
        pxr = Ring([(sbuf(p2, f"px{i}", [128, 512], F32), Res(f"px{i}")) for i in range(3)])
        pmr = Ring([(sbuf(p2, f"pm{i}", [128, 2, 2, 128], BF16), Res(f"pm{i}")) for i in range(6)])
        scr_ = Ring([(psum(p2, f"sc{i}", [128, 2, 2, 128], F32), Res(f"sc{i}")) for i in range(4)])
        pvr = Ring([(psum(p2, f"pv{i}", [128, 2, 128], F32), Res(f"pv{i}")) for i in range(4)])

        pair_no = 0
        for u in range(2):
            for gi in range(3):
                t = 2 * gi + u
                d = DILS[gi]
                n = 48 // d
                npseg = n // 3
                sl = pair_no % 2
                pair_no += 1
                numT = {gi: numTb[sl]}
                R_numT = {gi: R_numTb[sl]}
                kT, Vc, mk = kTb[sl], Vcb[sl], mkb[sl]
                RkT, RVc, Rmk = R_kTb[sl], R_Vcb[sl], R_mkb[sl]
                qH = (qAb[sl], qBb[sl])
                RqH = (R_qAb[sl], R_qBb[sl])
                RqZ = (R_qAz[sl], R_qBz[sl])
                S.dma("sp", qH[0][0:64, :], qT_s[t][0:64, :], r=[R_qs], w=[RqH[0]], sem=RqH[0])
                S.dma("sp", qH[1][64:128, :], qT_s[t][64:128, :], r=[R_qs], w=[RqH[1]], sem=RqH[1])
                S.dma("sp", kT[:], kT_s[t], r=[R_ks], w=[RkT], sem=RkT)
                Vsrc = V_s.rearrange("(j p r) c -> r p j c", p=128, r=d)
                for r in range(d):
                    for j0 in range(0, n, 8):
                        j1 = min(n, j0 + 8)
                        S.dma("sp", Vc[:, r * n + j0:r * n + j1, :], Vsrc[r][:, j0:j1, t * 128:(t + 1) * 128],
                              r=[R_vs], w=[RVc], sem=RVc)
                S.dma("sp", mk[:].rearrange("p a b -> p (a b)"), emask[t], w=[Rmk], sem=Rmk)
                S.flush_dma()

                def tokslice(pos0, cnt):
                    return slice(pos0 * d + r, pos0 * d + r + (cnt - 1) * d + 1, d) if d > 1 else slice(pos0, pos0 + cnt)

                prev = None
                pend = []
                LAG = 3
                blocks = [(r, j) for r in range(d) for j in range(n)]
                for bi in range(len(blocks) + LAG):
                    if bi < len(blocks):
                        r, j = blocks[bi]
                        tj = (j, (j + 1) % n)
                        if j + 1 < n:
                            runs = [(0, 128, tokslice(128 * j + 64, 128), tokslice(128 * j + 64, 128))]
                        else:
                            runs = [(0, 64, tokslice(128 * j + 64, 64), tokslice(128 * j + 64, 64)),
                                    (64, 64, tokslice(0, 64), tokslice(0, 64))]
                        if (j + 1) % npseg != 0:
                            mt_ = 0
                        elif j + 1 == npseg:
                            mt_ = 1
                        else:
                            mt_ = 2
                        sc, Rsc = scr_.nxt()
                        for h in range(2):
                            for kt in range(2):
                                ks = tokslice(128 * tj[kt], 128)
                                for (c0, ncol, qs, qcs) in runs:
                                    S.mm(sc[:, h, kt, c0:c0 + ncol], kT[:, ks],
                                         qH[h][:, qcs], True, True, [RkT, RqH[h], RqZ[h]], [Rsc])
                        px, Rpx = pxr.nxt()
                        pm, Rpm = pmr.nxt()
                        S.act(px[:], sc[:].rearrange("p a b c -> p (a b c)"), AF.Exp, [Rsc], [Rpx], scale=0.125)
                        S.tt("dve", pm[:].rearrange("p a b c -> p (a b c)"), px[:], mk[:, mt_, :], ALU.mult,
                             [Rpx, Rmk], [Rpm])
                        pend.append((r, j, tj, runs, pm, Rpm))
                    prev = pend.pop(0) if (len(pend) > LAG or (bi >= len(blocks) and pend)) else None
                    if prev is not None:
                        pr, pj, ptj, pruns, ppm, Rppm = prev
                        pv, Rpv = pvr.nxt()
                        for h in range(2):
                            rows = slice(h * 64, (h + 1) * 64)
                            for kt in range(2):
                                S.mm(pv[rows, 0, :], Vc[:, pr * n + ptj[kt], h * 64:(h + 1) * 64], ppm[:, h, kt, :],
                                     kt == 0, kt == 1, [RVc, Rppm], [Rpv])
                            for kt in range(2):
                                S.mm(pv[rows, 1, :], ones_b[:, 0:64], ppm[:, h, kt, :],
                                     kt == 0, kt == 1, [R_ones, Rppm], [Rpv])
                        for (c0, ncol, qs, _qcs) in pruns:
                            S.cp("act", numT[gi][:, qs], pv[:, 0, c0:c0 + ncol], [Rpv], [R_numT[gi]])
                            if gi == 0:
                                S.cp("dve", Dt[:, qs], pv[:, 1, c0:c0 + ncol], [Rpv, R_numT[gi]], [R_D])
                            else:
                                S.tt("dve", Dt[:, qs], Dt[:, qs], pv[:, 1, c0:c0 + ncol], ALU.add, [R_D, Rpv, R_numT[gi]], [R_D])
                assert not pend
                S.defer_dma(STQ, mixT_s[t], numT[gi][:], r=[R_numT[gi]], w=[R_mixs], sem=R_numT[gi])
            S.flush_dma()
            for q4 in range(4):
                cs = slice(q4 * 1536, (q4 + 1) * 1536)
                S.dma("sp" if q4 % 2 == 0 else STQ, rD_s[:, u, cs], Dt[:, cs], r=[R_D], w=[R_rDs], sem=R_D)
        S.barrier()

    if stop_after >= 3:
      with ExitStack() as p3:
        w_down_sb = sbuf(p3, "w_down_sb", [128, MT, D], BF16)
        R_wdown = Res("wdown")
        w_down_v = w_down.rearrange("(k p) n -> p k n", p=128)
        for kh in range(11):
            S.dma("pool", w_down_sb[:, kh * 2:(kh + 1) * 2, :], w_down_v[:, kh * 2:(kh + 1) * 2, :], w=[R_wdown], sem=R_wdown)
        wgr = Ring([(sbuf(p3, f"wg{i}", [128, KT, 2, 128], BF16), Res(f"wg{i}")) for i in range(3)])
        mtr = Ring([(sbuf(p3, f"mt{i}", [128, KT, CH], BF16), Res(f"mt{i}")) for i in range(1)])
        xr = Ring([(sbuf(p3, f"x3_{i}", [128, D], F32), Res(f"x3_{i}")) for i in range(4)])
        rdc = sbuf(p3, "rdc", [128, 2, CH], F32)
        R_rdc = Res("rdc")
        x1r = Ring([(sbuf(p3, f"x1_{i}", [128, 4, D], F32), Res(f"x1_{i}")) for i in range(2)])
        msr = Ring([(sbuf(p3, f"ms{i}", [128, D], F32), Res(f"ms{i}")) for i in range(3)])
        xnr = Ring([(sbuf(p3, f"xn3_{i}", [128, D], BF16), Res(f"xn3_{i}")) for i in range(4)])
        h2r = Ring([(sbuf(p3, f"h2T{i}", [128, KT, CH], BF16), Res(f"h2T{i}")) for i in range(2)])
        aT = sbuf(p3, "aT", [128, MT, CH], BF16)
        R_aT = Res("aT")
        sgr = Ring([(sbuf(p3, f"sg{i}", [128, CH], BF16), Res(f"sg{i}")) for i in range(2)])
        junk = sbuf(p3, "junk3", [128, D], BF16)
        R_junk = Res("junk3")
        nrm = Ring([(sbuf(p3, f"nrm{i}", [128, 12], F32), Res(f"nrm{i}")) for i in range(6)])
        tmor = Ring([(psum(p3, f"tmo{i}", [128, 512], F32), Res(f"tmo{i}")) for i in range(3)])
        tpr = Ring([(psum(p3, f"tp3_{i}", [128, KT, 128], BF16), Res(f"tp3_{i}")) for i in range(1)])
        gpr = Ring([(psum(p3, f"gp{i}", [128, 2, CH], F32), Res(f"gp{i}")) for i in range(2)])

        def rms_rstd(src_ap, Rsrc):
            nt, Rn = nrm.nxt()
            S.memset("dve", nt[:, 0:1], 0.0, [], [Rn])
            S.act(junk[:], src_ap, AF.Square, [Rsrc], [R_junk, Rn], accum=nt[:, 0:1])
            S.ts("dve", nt[:, 0:1], nt[:, 0:1], 1.0 / D, RMS_EPS, ALU.mult, ALU.add, [Rn], [Rn])
            rsqrt(nt[:, 1:2], nt[:, 0:1], nt[:, 2:3], Rn, Rn, Rn)
            return nt[:, 1:2], Rn

        def mix_pre(c, step, st):
            tok0 = c * CH
            if step == 0:
                st["mt"] = mtr.nxt()
                mt, Rmt = st["mt"]
                S.dma("sp", mt[:], mixT_s[:, :, tok0:tok0 + CH].rearrange("k p n -> p k n"), r=[R_mixs], w=[Rmt], sem=Rmt)
                S.dma("sp", rdc[:], rD_s[:, :, tok0:tok0 + CH], r=[R_rDs], w=[R_rdc], sem=R_rdc)
                st["xt"] = []
                for i in range(4):
                    xt, Rx = xr.nxt()
                    S.dma("sp", xt[:], xin[tok0 + i * 128: tok0 + (i + 1) * 128, :], w=[Rx], sem=Rx)
                    st["xt"].append((xt, Rx))
                S.recip(rdc[:], rdc[:], [R_rdc], [R_rdc])
                st["x1"] = x1r.nxt()
                st["xn"] = []
                st["nt"] = nrm.nxt()
                st["nt2"] = nrm.nxt()
                S.memset("dve", st["nt"][0][:, 0:4], 0.0, [], [st["nt"][1]])
                S.memset("dve", st["nt2"][0][:, 0:4], 0.0, [], [st["nt2"][1]])
            else:
                uu = step - 1
                mt, Rmt = st["mt"]
                rb = rdc[:, uu, :]
                rb3 = bass.AP(rb.tensor, rb.offset, [list(rb.ap[0]), [0, 3], list(rb.ap[1])])
                S.tt("dve", mt[:, uu:6:2, :], mt[:, uu:6:2, :], rb3, ALU.mult, [Rmt, R_rdc], [Rmt])

        def mix_a(c, i, st):
            mt, Rmt = st["mt"]
            x1, Rx1 = st["x1"]
            nt, Rn = st["nt"]
            for half in range(2):
                tmo, R_tmo = tmor.nxt()
                for k in range(KT):
                    S.mm(tmo[:, :], mt[:, k, i * 128:(i + 1) * 128],
                         w_out_sb[:, k, half * 512:(half + 1) * 512], k == 0, k == KT - 1, [Rmt, R_wout], [R_tmo])
                S.cp("act", x1[:, i, half * 512:(half + 1) * 512], tmo[:, :], [R_tmo], [Rx1])
            S.act(junk[:], x1[:, i, :], AF.Square, [Rx1], [R_junk, Rn], accum=nt[:, i:i + 1])

        def chain(c, step, st, key):
            nt, Rn = st[key]
            if step == 0:
                S.ts("dve", nt[:, 0:4], nt[:, 0:4], 1.0 / D, RMS_EPS, ALU.mult, ALU.add, [Rn], [Rn])
                rsqrt_a(nt[:, 4:8], nt[:, 0:4], nt[:, 8:12], Rn, Rn, Rn)
            else:
                rsqrt_it(nt[:, 4:8], nt[:, 0:4], nt[:, 8:12], Rn, Rn, Rn)

        def mix_b(c, i, st):
            seg = c // 4
            x1, Rx1 = st["x1"]
            nt, Rn = st["nt"]
            xt, Rx = st["xt"][i]
            S.stt("dve", x1[:, i, :], x1[:, i, :], nt[:, 4 + i:5 + i], gg1[:, seg, :], ALU.mult, ALU.mult,
                  [Rx1, Rn, R_gg1], [Rx1])
            S.tt("dve", x1[:, i, :], x1[:, i, :], xt[:], ALU.add, [Rx1, Rx], [Rx1])

        def mix_c(c, i, st):
            x1, Rx1 = st["x1"]
            nt2, Rn2 = st["nt2"]
            S.act(junk[:], x1[:, i, :], AF.Square, [Rx1], [R_junk, Rn2], accum=nt2[:, i:i + 1])

        def mix_e(c, i, st):
            x1, Rx1 = st["x1"]
            nt2, Rn2 = st["nt2"]
            xn, Rxn = xnr.nxt()
            S.act(xn[:], x1[:, i, :], AF.Identity, [Rx1, Rn2], [Rxn], scale=nt2[:, 4 + i:5 + i])
            st["xn"].append((xn, Rxn))

        def tr_tile(c, i, st):
            seg = c // 4
            if i == 0:
                st["h2"] = h2r.nxt()
            h2T, Rh2 = st["h2"]
            xn, Rxn = st["xn"][i]
            tp, Rtp = tpr.nxt()
            for k in range(KT):
                S.tr(tp[:, k, :], xn[:, k * 128:(k + 1) * 128], ident_b[:], [Rxn, R_identb], [Rtp])
            for k in range(KT):
                S.act(h2T[:, k, i * 128:(i + 1) * 128], tp[:, k, :], AF.Identity, [Rtp, R_s2, R_sh2], [Rh2],
                      bias=sh2[:, seg, k:k + 1], scale=s2[:, seg, k:k + 1])

        def gu_tile(c, m, st):
            h2T, Rh2 = st["h2"]
            wg, Rwg = wgr.nxt()
            S.dma("sp", wg[:].rearrange("p k g c -> p (k g c)"), wgu_s[m], r=[R_wgus], w=[Rwg], sem=Rwg)
            gp, Rgp = gpr.nxt()
            for g in range(2):
                for k in range(KT):
                    S.mm(gp[:, g, :], wg[:, k, g, :], h2T[:, k, :], k == 0, k == KT - 1, [Rwg, Rh2], [Rgp])
            sg, Rsg = sgr.nxt()
            S.act(sg[:], gp[:, 0, :], AF.Silu, [Rgp], [Rsg])
            S.tt("dve", aT[:, m, :], sg[:], gp[:, 1, :], ALU.mult, [Rsg, Rgp], [R_aT])

        def dn_tile(c, i, st):
            seg = c // 4
            tok0 = c * CH
            x1, Rx1 = st["x1"]
            ms, Rms = msr.nxt()
            for half in range(2):
                tmo, R_tmo = tmor.nxt()
                for k in range(MT):
                    S.mm(tmo[:, :], aT[:, k, i * 128:(i + 1) * 128],
                         w_down_sb[:, k, half * 512:(half + 1) * 512], k == 0, k == MT - 1, [R_aT, R_wdown], [R_tmo])
                S.cp("act", ms[:, half * 512:(half + 1) * 512], tmo[:, :], [R_tmo], [Rms])
            rstd, Rn = rms_rstd(ms[:], Rms)
            S.flush_dma()
            S.stt("dve", ms[:], ms[:], rstd, gg2[:, seg, :], ALU.mult, ALU.mult, [Rms, Rn, R_gg2], [Rms])
            S.tt("dve", ms[:], ms[:], x1[:, i, :], ALU.add, [Rms, Rx1], [Rms])
            S.defer_dma(STQ, yout[tok0 + i * 128: tok0 + (i + 1) * 128, :], ms[:], r=[Rms], w=[R_y], sem=Rms)

        GUS = {}
        DNS = {}

        def at(tbl, pos, f, *args):
            tbl.setdefault(pos, []).append((f, args))

        for step in range(3):
            at(GUS, step, mix_pre, step)
        for i in range(4):
            at(GUS, 3 + 2 * i, mix_a, i)
        at(GUS, 10, chain, 0, "nt")
        at(GUS, 11, chain, 1, "nt")
        for i in range(4):
            at(GUS, 12 + i, mix_b, i)
            at(GUS, 13 + i, mix_c, i)
        at(GUS, 17, chain, 0, "nt2")
        at(GUS, 18, chain, 1, "nt2")
        for i in range(4):
            at(GUS, 19 + (i + 1) // 2, mix_e, i)
        at(GUS, 21, tr_tile, 0)
        for i in range(1, 4):
            at(DNS, i - 1, tr_tile, i)

        def run_tbl(tbl, pos, c1, st):
            for f, args in tbl.get(pos, []):
                if f is chain:
                    f(c1, args[0], st, args[1])
                else:
                    f(c1, args[0], st)

        cur = {}
        for pos in range(MT):
            run_tbl(GUS, pos, 0, cur)
        for pos in range(4):
            run_tbl(DNS, pos, 0, cur)
        for c in range(NCH):
            nxt = {}
            for m in range(MT):
                gu_tile(c, m, cur)
                if m == 1:
                    S.flush_dma()
                if c + 1 < NCH:
                    run_tbl(GUS, m, c + 1, nxt)
            for i in range(4):
                dn_tile(c, i, cur)
                if c + 1 < NCH:
                    run_tbl(DNS, i, c + 1, nxt)
            cur = nxt

    S.flush_dma()
    S.emit()
    return nc


def _alibi_slopes():
    return (2.0 ** (-8.0 * np.arange(1, 13, dtype=np.float64) / 12)).astype(np.float64)


def _build_masks(joined):
    slopes = _alibi_slopes()
    m = np.zeros((6, 128, 3, 2, 2, 128), np.float32)
    rk = np.arange(128)[:, None]
    c = np.arange(128)[None, :]
    for t in range(6):
        d = DILS[t // 2]
        for h in range(2):
            sl = slopes[2 * t + h]
            for kt in range(2):
                rel = 64 + c - 128 * kt - rk
                valid = (np.abs(rel) <= NSIDE)
                e = np.exp(-sl * d * np.abs(rel)) * valid
                same = ((kt == 0) == (c < 64)) * np.ones_like(rel)
                for ty, flag in enumerate((None, joined, 0.0)):
                    if ty == 0:
                        m[t, :, ty, h, kt, :] = e
                    else:
                        m[t, :, ty, h, kt, :] = e * np.where(same > 0, 1.0, flag)
    return m.reshape(6, 128, 3 * 2 * 2 * 128)


_PROG = {}


def kernel(x_prompt, x_sample, c_prompt, c_sample, w_ada, b_ada, g_pre_mix, w_in, w_s, b_s,
           g_gmlp, w_out, g_post_mix, g_pre_ffn, w_gu, w_down, g_post_ffn):
    debug = bool(int(os.environ.get("KDEBUG", "0")))
    stop_after = int(os.environ.get("KSTOP", "3"))
    f = lambda a: np.ascontiguousarray(np.asarray(a, dtype=np.float32))
    x_prompt, x_sample, c_prompt, c_sample = f(x_prompt), f(x_sample), f(c_prompt), f(c_sample)
    pk = lambda g: np.ascontiguousarray(f(g).reshape(KT, 128).T)
    shared = {
        "w_ada": f(w_ada)[0], "b_ada": f(b_ada)[0],
        "g_pre_mix_pk": pk(g_pre_mix[0]), "g_pre_ffn_pk": pk(g_pre_ffn[0]),
        "g_post_mix": f(g_post_mix)[0], "g_post_ffn": f(g_post_ffn)[0],
        "w_in": f(w_in)[0], "w_s": f(w_s)[0], "b_s": f(b_s)[0].reshape(512), "g_gmlp": f(g_gmlp)[0],
        "w_out": f(w_out)[0], "w_gu": f(w_gu)[0], "w_down": f(w_down)[0],
        "ident": np.eye(128, dtype=np.float32),
    }
    sel = np.zeros((NSEG, NSEG, 128), np.float32)
    for s in range(NSEG):
        sel[s, s, :] = 1.0
    shared["sel"] = sel
    masks = {True: _build_masks(1.0), False: _build_masks(0.0)}
    in_maps = []
    for core in range(NCORES):
        if core < 4:
            xs = [x_prompt[core, :SEG], x_prompt[core, SEG:], x_sample[core]]
            cs = [c_prompt[core], c_prompt[core], c_sample[core]]
            joined = True
        else:
            b0 = 4 + 3 * (core - 4)
            xs = [x_sample[b0], x_sample[b0 + 1], x_sample[b0 + 2]]
            cs = [c_sample[b0], c_sample[b0 + 1], c_sample[b0 + 2]]
            joined = False
        m = dict(shared)
        m["xc"] = np.ascontiguousarray(np.concatenate(xs, axis=0))
        cc = np.stack(cs, axis=0)
        m["ct"] = np.ascontiguousarray(cc.reshape(NSEG, KT, 128).transpose(2, 1, 0))
        m["emask"] = masks[joined]
        in_maps.append(m)

    key = (debug, stop_after)
    nc = build_program(debug=debug, stop_after=stop_after)
    ncores_run = int(os.environ.get("KCORES", NCORES))
    res = run_bass_kernel_spmd(nc, in_maps[:ncores_run], core_ids=list(range(ncores_run)), trace=bool(int(os.environ.get('KTRACE', '0'))))
    if os.environ.get('KTRACE'):
        print('EXEC_TIME_NS', res.exec_time_ns)
    if ncores_run < NCORES:
        kernel.last_results = res.results
        return None
    if debug:
        kernel.last_results = res.results
    y_prompt = np.zeros((4, 2 * SEG, D), np.float32)
    y_sample = np.zeros((16, SEG, D), np.float32)
    for core in range(NCORES):
        y = np.asarray(res.results[core]["y"], dtype=np.float32)
        if core < 4:
            y_prompt[core] = y[:2 * SEG]
            y_sample[core] = y[2 * SEG:]
        else:
            b0 = 4 + 3 * (core - 4)
            for s in range(3):
                y_sample[b0 + s] = y[s * SEG:(s + 1) * SEG]
    return (y_prompt, y_sample)
```
